# Optimizing a Trainium2 kernel written in Bass

```python
import math
import jax, jax.numpy as jnp
from jax import lax
import numpy as np

D_MODEL = 1024
BATCH = 8
SEQ = 2048
DEPTH = 2
DEC_BATCH = 128
DEC_SEQ = 1
PAST_LEN = 16384
PAGE_SIZE = 128

N_META = 16
POOL_WIDTH = D_MODEL // 2
POOL_WINDOWS = (2, 4, 8, 16)
POOL_GROUPS = len(POOL_WINDOWS)
POOL_GROUP_IN = POOL_WIDTH // POOL_GROUPS
POOL_GROUP_OUT = D_MODEL // POOL_GROUPS
POOL_HIST = max(POOL_WINDOWS) - 1
SSM_WIDTH = D_MODEL // 2
SSM_GROUP_CH = 16
SSM_GROUPS = SSM_WIDTH // SSM_GROUP_CH
SSM_STATE = 64
D_FF = 2816
CONV_W = 3
RMS_EPS = 1e-6
DT_MIN = 1e-3
DT_MAX = 1e-1
IN_COLS = POOL_WIDTH + SSM_WIDTH + 2 * D_MODEL

kernel_name = "hybrid_pool_s5_convffn_decode_step"


def rmsnorm(x, g):
    xf = x.astype(jnp.float32)
    y = xf * lax.rsqrt(jnp.mean(xf * xf, axis=-1, keepdims=True) + RMS_EPS)
    return (y * g.astype(jnp.float32)).astype(x.dtype)


def pool_mixer(u, hist, start_pos, w_grp, scale):
    L = u.shape[1]
    ext = jnp.concatenate([hist.astype(u.dtype), u], axis=1)
    extf = ext.astype(jnp.float32)
    csum = jnp.pad(jnp.cumsum(extf, axis=1), ((0, 0), (1, 0), (0, 0)))
    ends = csum[:, POOL_HIST + 1:]
    pos = start_pos + jnp.arange(L)
    outs = []
    for g, w in enumerate(POOL_WINDOWS):
        sl = slice(g * POOL_GROUP_IN, (g + 1) * POOL_GROUP_IN)
        begin = csum[:, POOL_HIST + 1 - w:POOL_HIST + 1 - w + L, sl]
        cnt = jnp.minimum(w, pos + 1).astype(jnp.float32)[None, :, None]
        d = (ends[..., sl] - begin) / cnt - extf[:, POOL_HIST:, sl]
        outs.append(jnp.einsum("blc,cd->bld", d, w_grp[g].astype(jnp.float32)))
    y = jnp.concatenate(outs, axis=-1) * scale.astype(jnp.float32)
    return y.astype(u.dtype), ext[:, -POOL_HIST:]


def ssm_mixer(u, h0_re, h0_im, A_re, A_im, log_dt, B_re, B_im, C_re, C_im, D_skip):
    Bsz, L, _ = u.shape
    f32 = jnp.float32
    uf = u.astype(f32).reshape(Bsz, L, SSM_GROUPS, SSM_GROUP_CH)
    dt = jnp.exp(log_dt.astype(f32))[:, None]
    lam_re, lam_im = A_re.astype(f32), A_im.astype(f32)
    mag = jnp.exp(lam_re * dt)
    ab_re = mag * jnp.cos(lam_im * dt)
    ab_im = mag * jnp.sin(lam_im * dt)
    den = lam_re * lam_re + lam_im * lam_im
    nr, ni = ab_re - 1.0, ab_im
    q_re = (nr * lam_re + ni * lam_im) / den
    q_im = (ni * lam_re - nr * lam_im) / den
    Br, Bi = B_re.astype(f32), B_im.astype(f32)
    bb_re = q_re[..., None] * Br - q_im[..., None] * Bi
    bb_im = q_re[..., None] * Bi + q_im[..., None] * Br
    bu_re = jnp.einsum("blgp,gnp->blgn", uf, bb_re)
    bu_im = jnp.einsum("blgp,gnp->blgn", uf, bb_im)
    h0r, h0i = h0_re.astype(f32), h0_im.astype(f32)
    bu_re = bu_re.at[:, 0].add(ab_re * h0r - ab_im * h0i)
    bu_im = bu_im.at[:, 0].add(ab_re * h0i + ab_im * h0r)
    a_re = jnp.broadcast_to(ab_re, bu_re.shape)
    a_im = jnp.broadcast_to(ab_im, bu_im.shape)

    def combine(left, right):
        a1r, a1i, b1r, b1i = left
        a2r, a2i, b2r, b2i = right
        return (a1r * a2r - a1i * a2i, a1r * a2i + a1i * a2r,
                a2r * b1r - a2i * b1i + b2r, a2r * b1i + a2i * b1r + b2i)

    _, _, h_re, h_im = lax.associative_scan(combine, (a_re, a_im, bu_re, bu_im), axis=1)
    y = (jnp.einsum("blgn,gpn->blgp", h_re, C_re.astype(f32))
         - jnp.einsum("blgn,gpn->blgp", h_im, C_im.astype(f32)))
    y = y.reshape(Bsz, L, SSM_WIDTH) + D_skip.astype(f32) * u.astype(f32)
    return y.astype(u.dtype), h_re[:, -1].astype(h0_re.dtype), h_im[:, -1].astype(h0_im.dtype)


def glu_out(y, w_glu):
    z = jax.nn.gelu(y)
    zz = z @ w_glu
    return zz[..., :D_MODEL] * jax.nn.sigmoid(zz[..., D_MODEL:])


def conv_ffn(h, hist, w_up, conv_w, conv_b, w_down):
    L = h.shape[1]
    up = h @ w_up
    g, v = up[..., :D_FF], up[..., D_FF:]
    ext = jnp.concatenate([hist.astype(g.dtype), g], axis=1)
    c = conv_b + conv_w[0] * ext[:, 0:L]
    for k in range(1, CONV_W):
        c = c + conv_w[k] * ext[:, k:k + L]
    out = (jax.nn.gelu(c) * v) @ w_down
    return out, ext[:, -(CONV_W - 1):]


def trunk(x, st_pool, st_re, st_im, st_conv, start_pos, norm1_g, w_in, pool_w, pool_scale,
          ssm_A_re, ssm_A_im, ssm_log_dt, ssm_B_re, ssm_B_im, ssm_C_re, ssm_C_im, ssm_D,
          w_glu, w_out, norm2_g, w_up, conv_w, conv_b, w_down, norm_f_g):
    new_pool, new_re, new_im, new_conv = [], [], [], []
    o_ssm = POOL_WIDTH
    o_gp = POOL_WIDTH + SSM_WIDTH
    o_gs = o_gp + D_MODEL
    for l in range(DEPTH):
        h = rmsnorm(x, norm1_g[l])
        proj = h @ w_in[l]
        u_pool = proj[..., :o_ssm]
        u_ssm = proj[..., o_ssm:o_gp]
        gate_pool = jax.nn.sigmoid(proj[..., o_gp:o_gs])
        gate_ssm = jax.nn.sigmoid(proj[..., o_gs:])
        y_pool, hp = pool_mixer(u_pool, st_pool[l], start_pos, pool_w[l], pool_scale[l])
        y_s, hr, hi = ssm_mixer(u_ssm, st_re[l], st_im[l], ssm_A_re[l], ssm_A_im[l], ssm_log_dt[l],
                                ssm_B_re[l], ssm_B_im[l], ssm_C_re[l], ssm_C_im[l], ssm_D[l])
        y_ssm = glu_out(y_s, w_glu[l])
        x = x + (gate_pool * y_pool + gate_ssm * y_ssm) @ w_out[l]
        h2 = rmsnorm(x, norm2_g[l])
        f, hc = conv_ffn(h2, st_conv[l], w_up[l], conv_w[l], conv_b[l], w_down[l])
        x = x + f
        new_pool.append(hp)
        new_re.append(hr)
        new_im.append(hi)
        new_conv.append(hc)
    y = rmsnorm(x, norm_f_g)
    return y, jnp.stack(new_pool), jnp.stack(new_re), jnp.stack(new_im), jnp.stack(new_conv)


def setup_inputs(seed: int = 0) -> dict:
    key = jax.random.key(seed)
    ks = jax.random.split(key, 32)
    f32 = jnp.float32
    nrm = lambda k, shape, s: jax.random.normal(k, shape, f32) * s
    n_idx = jnp.arange(SSM_STATE, dtype=f32)
    return {
        "x_prompt": nrm(ks[0], (BATCH, SEQ, D_MODEL), 1.0),
        "x_sample": nrm(ks[1], (DEC_BATCH, DEC_SEQ, D_MODEL), 1.0),
        "state_pool": nrm(ks[2], (DEPTH, DEC_BATCH, POOL_HIST, POOL_WIDTH), 1.0),
        "state_ssm_re": nrm(ks[3], (DEPTH, DEC_BATCH, SSM_GROUPS, SSM_STATE), 0.5),
        "state_ssm_im": nrm(ks[4], (DEPTH, DEC_BATCH, SSM_GROUPS, SSM_STATE), 0.5),
        "state_conv": nrm(ks[5], (DEPTH, DEC_BATCH, CONV_W - 1, D_FF), 1.0),
        "meta_tokens": nrm(ks[6], (N_META, D_MODEL), 1.0),
        "norm1_g": 1.0 + nrm(ks[7], (DEPTH, D_MODEL), 0.01),
        "w_in": nrm(ks[8], (DEPTH, D_MODEL, IN_COLS), D_MODEL ** -0.5),
        "pool_w": nrm(ks[9], (DEPTH, POOL_GROUPS, POOL_GROUP_IN, POOL_GROUP_OUT), POOL_GROUP_IN ** -0.5),
        "pool_scale": 1.0 + nrm(ks[10], (DEPTH, D_MODEL), 0.1),
        "ssm_A_re": -0.5 + nrm(ks[11], (DEPTH, SSM_GROUPS, SSM_STATE), 0.01),
        "ssm_A_im": math.pi * n_idx + nrm(ks[12], (DEPTH, SSM_GROUPS, SSM_STATE), 0.01),
        "ssm_log_dt": jax.random.uniform(ks[13], (DEPTH, SSM_GROUPS), f32,
                                         math.log(DT_MIN), math.log(DT_MAX)),
        "ssm_B_re": nrm(ks[14], (DEPTH, SSM_GROUPS, SSM_STATE, SSM_GROUP_CH), SSM_GROUP_CH ** -0.5),
        "ssm_B_im": nrm(ks[15], (DEPTH, SSM_GROUPS, SSM_STATE, SSM_GROUP_CH), SSM_GROUP_CH ** -0.5),
        "ssm_C_re": nrm(ks[16], (DEPTH, SSM_GROUPS, SSM_GROUP_CH, SSM_STATE), SSM_STATE ** -0.5),
        "ssm_C_im": nrm(ks[17], (DEPTH, SSM_GROUPS, SSM_GROUP_CH, SSM_STATE), SSM_STATE ** -0.5),
        "ssm_D": nrm(ks[18], (DEPTH, SSM_WIDTH), 1.0),
        "w_glu": nrm(ks[19], (DEPTH, SSM_WIDTH, 2 * D_MODEL), SSM_WIDTH ** -0.5),
        "w_out": nrm(ks[20], (DEPTH, D_MODEL, D_MODEL), D_MODEL ** -0.5),
        "norm2_g": 1.0 + nrm(ks[21], (DEPTH, D_MODEL), 0.01),
        "w_up": nrm(ks[22], (DEPTH, D_MODEL, 2 * D_FF), D_MODEL ** -0.5),
        "conv_w": nrm(ks[23], (DEPTH, CONV_W, D_FF), CONV_W ** -0.5),
        "conv_b": nrm(ks[24], (DEPTH, D_FF), 0.02),
        "w_down": nrm(ks[25], (DEPTH, D_FF, D_MODEL), D_FF ** -0.5),
        "norm_f_g": 1.0 + nrm(ks[26], (D_MODEL,), 0.01),
    }


def reference(x_prompt, x_sample, state_pool, state_ssm_re, state_ssm_im, state_conv, meta_tokens,
              norm1_g, w_in, pool_w, pool_scale, ssm_A_re, ssm_A_im, ssm_log_dt, ssm_B_re, ssm_B_im,
              ssm_C_re, ssm_C_im, ssm_D, w_glu, w_out, norm2_g, w_up, conv_w, conv_b, w_down, norm_f_g):
    dt = x_prompt.dtype
    bp = x_prompt.shape[0]
    meta = jnp.broadcast_to(meta_tokens.astype(dt)[None], (bp, N_META, D_MODEL))
    xp = jnp.concatenate([meta, x_prompt], axis=1)
    z_pool = jnp.zeros((DEPTH, bp, POOL_HIST, POOL_WIDTH), dt)
    z_ssm = jnp.zeros((DEPTH, bp, SSM_GROUPS, SSM_STATE), dt)
    z_conv = jnp.zeros((DEPTH, bp, CONV_W - 1, D_FF), dt)
    yp, pool_p, re_p, im_p, conv_p = trunk(
        xp, z_pool, z_ssm, z_ssm, z_conv, 0, norm1_g, w_in, pool_w, pool_scale,
        ssm_A_re, ssm_A_im, ssm_log_dt, ssm_B_re, ssm_B_im, ssm_C_re, ssm_C_im, ssm_D,
        w_glu, w_out, norm2_g, w_up, conv_w, conv_b, w_down, norm_f_g)
    y_prompt = yp[:, N_META:]
    y_sample, pool_s, re_s, im_s, conv_s = trunk(
        x_sample, state_pool, state_ssm_re, state_ssm_im, state_conv, PAST_LEN,
        norm1_g, w_in, pool_w, pool_scale,
        ssm_A_re, ssm_A_im, ssm_log_dt, ssm_B_re, ssm_B_im, ssm_C_re, ssm_C_im, ssm_D,
        w_glu, w_out, norm2_g, w_up, conv_w, conv_b, w_down, norm_f_g)
    return (y_prompt, y_sample, pool_p, re_p, im_p, conv_p, pool_s, re_s, im_s, conv_s)
```

```python
import numpy as np
from contextlib import ExitStack
import concourse.bass as bass
import concourse.mybir as mybir
from concourse.bass_utils import run_bass_kernel_spmd

F32 = mybir.dt.float32
BF16 = mybir.dt.bfloat16
I32 = mybir.dt.int32
ALU = mybir.AluOpType
AF = mybir.ActivationFunctionType
AX = mybir.AxisListType

ENGS = ["pe", "act", "dve", "pool", "sp"]
NCORES = 8
D = 1024
KT = 8
L = 2
DFF = 2816
FT = 22
NMETA = 16
SEQ = 2048
NPR = NMETA + SEQ
NSM = 16
NTOK = NPR + NSM
NBLK = 3
CH = 86
TBP = CH * 8
TWO_PI = float(2.0 * np.pi)
NK = 19
KLIST = [0, 1, 2, 3, 4, 5, 6, 7, 8, -8, -7, 7, 6, 5, 4, 3, 2, 1, 0]
V_N1, V_N2, V_NF, V_PSC, V_SSD, V_CW, V_CB, V_END = 0, 16, 32, 40, 56, 64, 196, 240


class StopGen(Exception):
    pass


class Sched:
    def __init__(self, nc, es, n_dma_sems=44):
        self.nc = nc
        self.dry = False
        self.ops = {e: [] for e in ENGS}
        self.cnt = {e: 0 for e in ENGS}
        self.esem = {e: es.enter_context(nc.semaphore("s_" + e)) for e in ENGS}
        self.free_dma_sems = [es.enter_context(nc.semaphore("d%d" % i)) for i in range(n_dma_sems)]
        self.dma_sem = {}
        self.last_w = {}
        self.readers = {}
        self.waited = {e: {} for e in ENGS}
        self.out_events = []

    def _need(self, eng, ev, waits):
        if ev is None:
            return
        sem, val, src = ev
        if src == eng and eng == "pe":
            return
        if isinstance(src, tuple):
            val = max(val, self.dma_sem[src[1]][1])
        w = self.waited[eng]
        if w.get(sem.name, 0) >= val:
            return
        w[sem.name] = val
        waits.append((sem, val))

    def _deps(self, eng, reads, writes):
        waits = []
        for k in reads:
            self._need(eng, self.last_w.get(k), waits)
        for k in writes:
            self._need(eng, self.last_w.get(k), waits)
            for ev in self.readers.get(k, []):
                self._need(eng, ev, waits)
        return waits

    def _commit(self, ev, reads, writes):
        for k in reads:
            self.readers.setdefault(k, []).append(ev)
        for k in writes:
            self.last_w[k] = ev
            self.readers[k] = []

    calls = 0
    limit = None

    def _count(self):
        self.calls += 1
        if self.limit is not None and self.calls > self.limit:
            raise StopGen()

    def op(self, eng, fns, reads=(), writes=()):
        self._count()
        if self.dry:
            return None
        if callable(fns):
            fns = [fns]
        waits = self._deps(eng, reads, writes)
        self.cnt[eng] += 1
        ev = (self.esem[eng], self.cnt[eng], eng)
        self.ops[eng].append((waits, fns, (self.esem[eng], 1)))
        self._commit(ev, reads, writes)
        return ev

    def dma(self, eng, out, in_, reads=(), writes=(), key=None, out_final=False):
        self._count()
        if self.dry:
            return None
        if key is None:
            key = writes[0]
        if key not in self.dma_sem:
            self.dma_sem[key] = [self.free_dma_sems.pop(), 0]
        ent = self.dma_sem[key]
        waits = self._deps(eng, reads, writes)
        ent[1] += 16
        ev = (ent[0], ent[1], ("dma", key))
        self.ops[eng].append((waits, [lambda e: e.dma_start(out=out, in_=in_)], (ent[0], 16)))
        self._commit(ev, reads, writes)
        if out_final:
            self.out_events.append(ev)
        return ev

    def dma_more(self, eng, out, in_, key):
        self._count()
        if self.dry:
            return None
        ent = self.dma_sem[key]
        ent[1] += 16
        ev = (ent[0], ent[1], ("dma", key))
        self.ops[eng].append(([], [lambda e: e.dma_start(out=out, in_=in_)], (ent[0], 16)))
        for k, v in list(self.last_w.items()):
            if v[0] is ent[0] and isinstance(v[2], tuple) and v[2][1] == key:
                self.last_w[k] = ev
        return ev

    def emit(self):
        nc = self.nc
        finals = {}
        for ev in self.out_events:
            cur = finals.get(ev[0].name)
            if cur is None or cur[1] < ev[1]:
                finals[ev[0].name] = (ev[0], ev[1])
        ops = self.ops

        def run(eng_name):
            def f(e):
                for waits, fns, inc in ops[eng_name]:
                    for sem, val in waits:
                        e.wait_ge(sem, val)
                    ins = None
                    for fn in fns:
                        ins = fn(e)
                    ins.then_inc(inc[0], inc[1])
                if eng_name == "sp":
                    for sem, val in finals.values():
                        e.wait_ge(sem, val)
            return f

        with nc.Block() as block:
            block.sync(run("sp"))
            block.scalar(run("act"))
            block.vector(run("dve"))
            block.gpsimd(run("pool"))
            block.tensor(run("pe"))


def I_tt(out, a, b, op):
    return lambda e: e.tensor_tensor(out=out, in0=a, in1=b, op=op)


def I_ts(out, a, s1, op0, s2=None, op1=None):
    if op1 is None:
        return lambda e: e.tensor_scalar(out=out, in0=a, scalar1=s1, scalar2=None, op0=op0)
    return lambda e: e.tensor_scalar(out=out, in0=a, scalar1=s1, scalar2=s2, op0=op0, op1=op1)


def I_stt(out, a, s, b, op0, op1):
    return lambda e: e.scalar_tensor_tensor(out=out, in0=a, scalar=s, in1=b, op0=op0, op1=op1)


def I_act(out, a, func, **kw):
    return lambda e: e.activation(out=out, in_=a, func=func, **kw)


def I_cp(out, a):
    return lambda e: e.tensor_copy(out=out, in_=a)


def I_acp(out, a):
    return lambda e: e.copy(out=out, in_=a)


def I_mm(out, lhsT, rhs, start, stop):
    return lambda e: e.matmul(out, lhsT=lhsT, rhs=rhs, start=start, stop=stop)


def I_tr(out, in_, ident):
    return lambda e: e.transpose(out, in_, ident)


def I_memset(out, v):
    return lambda e: e.memset(out, v)


def I_memzero(out):
    return lambda e: e.memzero(out)


def I_scan(out, d0, d1, init):
    return lambda e: e.tensor_tensor_scan(out=out, data0=d0, data1=d1, initial=init, op0=ALU.mult, op1=ALU.add)


def I_red(out, a):
    return lambda e: e.tensor_reduce(out=out, in_=a, axis=AX.X, op=ALU.add)


def I_recip(out, a):
    return lambda e: e.reciprocal(out=out, in_=a)


class Alloc:
    def __init__(self, nc):
        self.nc = nc
        self.base = (nc.sbuf_base + 63) // 64 * 64
        self.top = nc.sbuf_top
        self.cur = self.base

    def _sz(self, shape, dt):
        n = 1
        for s in shape[1:]:
            n *= s
        b = 2 if dt == BF16 else 4
        return (n * b + 63) // 64 * 64

    def new(self, name, shape, dt):
        off = self.cur
        self.off = getattr(self, "off", {})
        self.off[name] = off
        self.cur += self._sz(shape, dt)
        assert self.cur <= self.top, ("SBUF overflow", name, self.cur, self.top)
        return self.nc.alloc_sbuf_tensor_at(name, shape, dt, offset=off)

    def at(self, name, shape, dt, off):
        assert off + self._sz(shape, dt) <= self.top, ("SBUF overflow", name, off, self._sz(shape, dt), self.top)
        return self.nc.alloc_sbuf_tensor_at(name, shape, dt, offset=off), off + self._sz(shape, dt)


def build_program(stage=99, limit=None):
    nc = bass.Bass("TRN2", target_bir_lowering=False)

    def chk(n):
        if stage < n:
            raise StopGen()

    dram_in = lambda n, s, dt=F32: nc.dram_tensor(n, list(s), dt, kind="ExternalInput").ap()
    dram_out = lambda n, s, dt=F32: nc.dram_tensor(n, list(s), dt, kind="ExternalOutput").ap()
    x_tT = dram_in("x_tT", [D, NTOK])
    w_in = dram_in("w_in", [L, D, 3072])
    pool_w = dram_in("pool_w", [L, 4, 128, 256])
    w_glu = dram_in("w_glu", [L, 512, 2048])
    w_out = dram_in("w_out", [L, D, D])
    w_up = dram_in("w_up", [L, D, 2 * DFF])
    w_down = dram_in("w_down", [L, DFF, D])
    vecs_d = dram_in("vecs", [128, V_END])
    ssmA_d = dram_in("ssmA", [128, L, 3, 16])
    ssmB_d = dram_in("ssmB", [128, L, 2, 16, 16])
    ssmC_d = dram_in("ssmC", [128, L, 2, 16, 16])
    stp_d = dram_in("stp", [128, L, 4, 16, 15])
    sth_d = dram_in("sth", [128, L, 2, 16, 16])
    stc_d = dram_in("stc", [128, L, FT, 16, 2])
    sp_raw = dram_in("sp_raw", [L, 16, 15, 512])
    sc_raw = dram_in("sc_raw", [L, 16, 2, DFF])
    ident_d = dram_in("ident", [128, 128])
    mask_d = dram_in("mask", [128, 128])
    kvec_d = dram_in("kvec", [128, 16, NK])
    cvec_d = dram_in("cvec", [128, 16, CH])
    invc_d = dram_in("invc", [128, 4, 16])

    y_allT = dram_out("y_allT", [D, NTOK])
    o_pp = dram_out("o_pp", [128, L, 4, 15])
    o_psn = dram_out("o_psn", [128, L, 4, 16])
    o_psh = dram_out("o_psh", [L, 16, 15, 512])
    o_hp = dram_out("o_hp", [128, L, 2, 16])
    o_hs = dram_out("o_hs", [128, L, 2, 16, 16])
    o_cp = dram_out("o_cp", [128, L, FT, 2])
    o_csn = dram_out("o_csn", [128, L, FT, 16])
    o_csh = dram_out("o_csh", [L, 16, 2, DFF])
    scr1 = nc.dram_tensor("scr1", [512, 8, 102], BF16, kind="Internal").ap()
    scr2 = nc.dram_tensor("scr2", [512, 8, 102], BF16, kind="Internal").ap()

    es = ExitStack()
    S = Sched(nc, es)
    A = Alloc(nc)
    NTM = TBP + NSM
    CB = CH + NSM

    xT = A.new("xT", [128, KT, NTM], F32)
    hT = A.new("hT", [128, KT, NTM], BF16)
    rstd = A.new("rstd", [128, NTM], F32)
    vecs = A.new("vecs", [128, V_END], F32)
    ident = A.new("ident", [128, 128], F32)
    maskt = A.new("maskt", [128, 128], F32)
    invc = A.new("invc", [128, 4, 16], F32)
    ones = A.new("ones", [128, 128], BF16)
    epsb = A.new("epsb", [128, 1], F32)
    Toep = [A.new("Toep%d" % l, [128, 32, 128], BF16) for l in range(L)]
    Bin = [A.new("Bin%d" % l, [128, 32, 2, 64], BF16) for l in range(L)]
    Cout = [A.new("Cout%d" % l, [128, 16, 2, 128], BF16) for l in range(L)]
    Er = [A.new("Er%d" % l, [128, 16, CH], F32) for l in range(L)]
    Ei = [A.new("Ei%d" % l, [128, 16, CH], F32) for l in range(L)]
    sml = [A.new("sml%d" % l, [128, 7, 16], F32) for l in range(L)]
    Hend = A.new("Hend", [128, L, 2, 16], F32)
    hinit = A.new("hinit", [128, 2, 16], F32)
    tini = A.new("tini", [128, 2, 16], F32)
    uhist = A.new("uhist", [128, L, 4, 16], F32)
    ghist = A.new("ghist", [128, L, FT, 2], F32)
    wW = [A.new("wW%d" % i, [128, KT, 256], BF16) for i in range(3)]
    zone = A.cur
    upT = A.new("upT", [128, 4, 16 + NTM], BF16)
    dT = A.new("dT", [128, 4, NTM], BF16)
    usT = A.new("usT", [128, 4, 8, CB], BF16)
    U8 = A.new("U8", [128, 32, CB], BF16)
    Gt = A.new("Gt", [128, 2, 16, CH], F32)
    Rt = A.new("Rt", [128, 16, CH], F32)
    X0s = A.new("X0s", [128, 2, 16, NSM], F32)
    hiR = A.new("hiR", [128, 2, 16], F32)
    T1 = A.new("T1", [128, 8, CH], F32)
    T2 = A.new("T2", [128, 8, CH], F32)
    Hp = A.new("Hp", [128, 2, 2, 16, CB], BF16)
    gy = A.new("gy", [128, 4, NTM], BF16)
    sg = [A.new("sg%d" % i, [128, 3, NTM // 2], F32) for i in range(2)]
    usn = A.new("usn", [128, 4, 16], F32)
    wG = [A.new("wG%d" % i, [128, 4, 256], BF16) for i in range(2)]
    wP = A.new("wP", [128, 4, 256], BF16)
    hsT = A.new("hsT", [128, 4, 16, 15], F32)
    hsum = A.new("hsum", [128, 4, 16], F32)
    mix_end = A.cur
    hs_t, _ = A.at("hs_t", [128, 4, 16, 16], F32, A.off["sg1"])
    h0s, _o = A.at("h0s", [128, 2, 16, 16], F32, A.off["sg0"])
    hs_new, _o = A.at("hs_new", [128, 2, 16, 16], F32, _o)
    assert _o <= A.off["sg0"] + A._sz([128, 3, NTM // 2], F32)
    ysg = []
    _o = A.off["Hp"]
    for i in range(2):
        t, _o = A.at("ysg%d" % i, [128, NTM], F32, _o)
        ysg.append(t)
    assert _o <= A.off["Hp"] + A._sz([128, 2, 2, 16, CB], BF16)
    gt_span = A._sz([128, 2, 16, CH], F32) + A._sz([128, 16, CH], F32)
    Y8, nxt = A.at("Y8", [128, 32, CB], BF16, A.off["Gt"])
    ysT, nxt = A.at("ysT", [128, 4, 8, CB], BF16, nxt)
    assert nxt <= A.off["Gt"] + gt_span
    sq, nxt2 = A.at("sq", [128, KT, NTM], BF16, A.off["Gt"])
    assert nxt2 <= A.off["Gt"] + gt_span
    rt, nxt2 = A.at("rt", [128, NTM], F32, A.off["T1"])
    assert nxt2 <= A.off["T1"] + 2 * A._sz([128, 8, CH], F32)
    mg, nxt = A.at("mg", [128, KT, NTM], BF16, A.off["usT"])
    assert nxt <= A.off["Gt"], (nxt, A.off["Gt"])
    cur = zone
    a_ff, cur = A.at("a_ff", [128, FT, NTM], BF16, cur)
    gS = []
    for i in range(2):
        t, cur = A.at("gS%d" % i, [128, 2 + NTM], F32, cur)
        gS.append(t)
    cbf = []
    for i in range(2):
        t, cur = A.at("cb%d" % i, [128, NTM], F32, cur)
        cbf.append(t)
    gcb = []
    for i in range(2):
        t, cur = A.at("gc%d" % i, [128, NTM], F32, cur)
        gcb.append(t)
    vS = []
    for i in range(2):
        t, cur = A.at("vS%d" % i, [128, NTM], F32, cur)
        vS.append(t)
    wD = []
    for i in range(2):
        t, cur = A.at("wD%d" % i, [128, FT, 128], BF16, cur)
        wD.append(t)
    stcs, cur = A.at("stcs", [128, FT, 16, 2], F32, cur)
    ffn_end = cur
    yT, cur2 = A.at("yT", [128, KT, NTM], F32, zone)
    ystg = []
    tok_in = []
    for i in range(2):
        t_, _ = A.at("tok_in%d" % i, [128, D], F32, cur2)
        tok_in.append(t_)
        t, cur2 = A.at("ystg%d" % i, [128, D], F32, cur2)
        ystg.append(t)
    hstg = []
    cur3 = zone
    zoff = {}

    def ztmp(name, shape, dt=F32):
        nonlocal cur3
        zoff[name] = cur3
        t, cur3 = A.at(name, shape, dt, cur3)
        return t
    sA = ztmp("sA", [128, 3, 16])
    kk = ztmp("kk", [128, 16, NK])
    cc = ztmp("cc", [128, 16, CH])
    t16 = [ztmp("t16_%d" % i, [128, 16]) for i in range(12)]
    pk = [ztmp("pk%d" % i, [128, 16, NK]) for i in range(6)]
    pki = ztmp("pki", [128, 16, NK], I32)
    PWr = ztmp("PWr", [128, 16, NK])
    PWi = ztmp("PWi", [128, 16, NK])
    sB = ztmp("sB", [128, 2, 16, 16])
    sC = ztmp("sC", [128, 2, 16, 16])
    bbr = ztmp("bbr", [128, 16, 16])
    bbi = ztmp("bbi", [128, 16, 16])
    BTr = ztmp("BTr", [128, 16, 128])
    BTi = ztmp("BTi", [128, 16, 128])
    Ccr = ztmp("Ccr", [128, 16, 128])
    Cci = ztmp("Cci", [128, 16, 128])
    RCr = ztmp("RCr", [128, 16, 128])
    RCn = Ccr
    Z1 = ztmp("Z1", [128, 16, 128])
    Z2 = ztmp("Z2", [128, 16, 128])
    RCbr, _ = A.at("RCbr", [128, 16, 2, 128], BF16, zoff["Z1"])
    RCbn, _ = A.at("RCbn", [128, 16, 2, 128], BF16, zoff["Z2"])
    BTrb, _o2 = A.at("BTrb", [128, 16, 128], BF16, zoff["Cci"])
    BTib, _o2 = A.at("BTib", [128, 16, 128], BF16, _o2)
    ec0_ap = Z1[:, :, 0:CH]
    eci_ap = Z2[:, :, 0:CH].bitcast(I32)
    for i in range(2):
        t, cur3 = A.at("hstg%d" % i, [128, 512], F32, cur3)
        hstg.append(t)
    assert max(mix_end, ffn_end, cur2, cur3) <= A.top, (mix_end, ffn_end, cur2, cur3, A.top)
    ZONE = "ZONE"

    banks = [nc.alloc_psum_tensor("pb%d" % i, [128, 512], F32) for i in range(8)]
    bank_i = [0]

    def pbank():
        i = bank_i[0] % 8
        bank_i[0] += 1
        return banks[i], "pb%d" % i

    phase = [0]
    ZPOOLS = ("G", "P", "D")

    class WS:
        def __init__(self):
            self.plan = {}
            self.pos = {}
            self.issued = {}
            self.slots = {"W": wW, "G": wG, "P": [wP], "D": wD}

        def get(self, pool, loads):
            if S.dry:
                self.plan.setdefault(pool, []).append((loads, phase[0]))
                return self.slots[pool][0], pool + "0"
            i = self.pos.get(pool, 0)
            self.pos[pool] = i + 1
            n = len(self.slots[pool])
            self._issue_upto(pool, i)
            slot = i % n
            return self.slots[pool][slot], "%s%d" % (pool, slot)

        def _issue_upto(self, pool, upto):
            n = len(self.slots[pool])
            j = self.issued.get(pool, 0)
            while j <= upto and j < len(self.plan[pool]):
                loads, ph = self.plan[pool][j]
                if pool in ZPOOLS and ph != phase[0]:
                    break
                slot = j % n
                for li, (dst_fn, src) in enumerate(loads):
                    if li == 0:
                        S.dma("pool", dst_fn(self.slots[pool][slot]), src, reads=([ZONE] if pool in ZPOOLS else []),
                              writes=["%s%d" % (pool, slot)])
                    else:
                        S.dma_more("pool", dst_fn(self.slots[pool][slot]), src, "%s%d" % (pool, slot))
                j += 1
            self.issued[pool] = j

        def prefetch(self, pool):
            if S.dry:
                return
            n = len(self.slots[pool])
            i = self.pos.get(pool, 0)
            self._issue_upto(pool, i + n - 1)

        def new_phase(self):
            if S.dry:
                return
            for pool in ZPOOLS:
                if pool in self.plan:
                    n = len(self.slots[pool])
                    self._issue_upto(pool, self.pos.get(pool, 0) + n - 1)

    W = WS()

    V = lambda c0, n=1: vecs[:, c0:c0 + n]

    def fence(extra=()):
        S.op("dve", [lambda e: e.memset(T1[:, 0, 0:1], 0.0)], reads=[], writes=list(extra) + [ZONE])
        phase[0] += 1
        W.new_phase()

    def history_copies():
        spv = sp_raw.rearrange("l b r c -> l (b r) c")
        opv = o_psh.rearrange("l b r c -> l (b r) c")
        scv = sc_raw.rearrange("l b r (q c) -> l (b r q) c", c=256)
        ocv = o_csh.rearrange("l b r (q c) -> l (b r q) c", c=256)
        pieces = []
        for l in range(L):
            for r0 in (0, 120):
                pieces.append((spv[l, r0:r0 + 120, :], opv[l, r0:r0 + 120, :], 120, 512))
            for r0, n in ((0, 128), (128, 128), (256, 96)):
                pieces.append((scv[l, r0:r0 + n, :], ocv[l, r0:r0 + n, :], n, 256))
        for i, (src, dst, n, w) in enumerate(pieces):
            hb, hk = hstg[i % 2], "hstg%d" % (i % 2)
            S.dma("sp", hb[0:n, 0:w], src, writes=[hk])
            S.dma("sp", dst, hb[0:n, 0:w], reads=[hk], writes=["o_hist"], key="outs", out_final=True)

    def gen():
        bank_i[0] = 0
        phase[0] = 0
        S.dma("sp", vecs[:], vecs_d, writes=["vecs"])
        chk(0)
        S.dma("sp", ident[:], ident_d, writes=["ident"])
        S.dma("sp", maskt[:], mask_d, writes=["maskt"])
        S.dma("sp", invc[:], invc_d, writes=["invc"])
        chk(0.2)
        S.op("dve", [I_memset(ones[:], 1.0), I_memset(epsb[:], 1e-6), I_memset(Hend[:], 0.0),
                     I_memset(uhist[:], 0.0), I_memset(ghist[:], 0.0), I_memset(hinit[:], 0.0)],
             writes=["ones", "epsb", "Hend", "uhist", "ghist", "hinit"])
        chk(0.4)
        chk(0.6)
        chk(1)
        load_x(0, TBP)
        setup_ssm()
        chk(2)
        for b in range(NBLK):
            nt = TBP + (NSM if b == NBLK - 1 else 0)
            chk(3)
            for l in range(L):
                layer(b, l, nt)
                chk(9 if b == 0 else 10 + b - 0.5 + 0.1 * l)
            final_out(b, nt)
            chk(10 + b)

    def setup_ssm():
        S.dma("sp", kk[:], kvec_d, writes=["kk"])
        S.dma("sp", cc[:], cvec_d, writes=["cc"])
        for l in range(L):
            S.dma("sp", sA[:], ssmA_d[:, l], writes=["sA"])
            S.dma("sp", sB[:], ssmB_d[:, l], writes=["sB"])
            S.dma("sp", sC[:], ssmC_d[:, l], writes=["sC"])
            if l == 0:
                history_copies()
            dt_, lrd, lid, r1 = t16[0], t16[1], t16[2], t16[3]
            S.op("act", I_act(dt_[:], sA[:, 2, :], AF.Exp), reads=["sA"], writes=["dt"])
            S.op("dve", [I_tt(lrd[:], sA[:, 0, :], dt_[:], ALU.mult)], reads=["sA", "dt"], writes=["lrd"])
            S.op("dve", [I_ts(lid[:], sA[:, 1, :], 1.0 / TWO_PI, ALU.mult)], reads=["sA"], writes=["lid0"])
            S.op("dve", [I_tt(lid[:], lid[:], dt_[:], ALU.mult)], reads=["lid0", "dt"], writes=["lid"])
            bc = lambda t: t[:].unsqueeze(2).to_broadcast([128, 16, NK])
            S.op("dve", [I_tt(pk[0][:], kk[:], bc(lrd), ALU.mult)], reads=["kk", "lrd"], writes=["pk0"])
            S.op("act", I_act(pk[0][:], pk[0][:], AF.Exp), reads=["pk0"], writes=["pk0e"])
            S.op("dve", [I_tt(pk[1][:], kk[:], bc(lid), ALU.mult)], reads=["kk", "lid"], writes=["pk1"])
            S.op("dve", [I_cp(pki[:], pk[1][:])], reads=["pk1"], writes=["pki"])
            S.op("dve", [I_cp(pk[2][:], pki[:])], reads=["pki"], writes=["pk2"])
            S.op("dve", [I_tt(pk[2][:], pk[1][:], pk[2][:], ALU.subtract)], reads=["pk1", "pk2"], writes=["pk2f"])
            S.op("act", I_act(pk[3][:], pk[2][:], AF.Sin, scale=TWO_PI), reads=["pk2f"], writes=["sin"])
            S.op("dve", [I_ts(pk[4][:], pk[1][:], 0.25, ALU.add)], reads=["pk1"], writes=["pk4"])
            S.op("dve", [I_cp(pki[:], pk[4][:])], reads=["pk4", "pk2"], writes=["pki2"])
            S.op("dve", [I_cp(pk[5][:], pki[:])], reads=["pki2"], writes=["pk5"])
            S.op("dve", [I_tt(pk[5][:], pk[4][:], pk[5][:], ALU.subtract)], reads=["pk4", "pk5"], writes=["pk5f"])
            S.op("act", I_act(pk[4][:], pk[5][:], AF.Sin, scale=TWO_PI), reads=["pk5f"], writes=["cos"])
            S.op("dve", [I_tt(PWr[:], pk[0][:], pk[4][:], ALU.mult)], reads=["pk0e", "cos"], writes=["PWr"])
            S.op("dve", [I_tt(PWi[:], pk[0][:], pk[3][:], ALU.mult)], reads=["pk0e", "sin"], writes=["PWi"])
            chk(1.1)
            sm = sml[l]
            SK = "sml%d" % l
            S.op("dve", [I_cp(sm[:, 0, :], pk[0][:, :, 8]), I_cp(sm[:, 1, :], pk[4][:, :, 8]), I_cp(sm[:, 2, :], pk[3][:, :, 8])],
                 reads=["pk0e", "cos", "sin"], writes=[SK + "a"])
            S.op("dve", [I_cp(sm[:, 3, :], PWr[:, :, 1]), I_cp(sm[:, 4, :], PWi[:, :, 1]),
                         I_cp(sm[:, 5, :], PWr[:, :, 10]), I_cp(sm[:, 6, :], PWi[:, :, 10])],
                 reads=["PWr", "PWi"], writes=[SK + "b"])
            f8 = pk[2][:, :, 8]
            bce = lambda ap: ap.unsqueeze(2).to_broadcast([128, 16, CH])
            EK = "E%d" % l
            S.op("dve", [I_tt(Er[l][:], cc[:], bce(f8), ALU.mult)], reads=["cc", "pk2f"], writes=[EK + "t"])
            S.op("dve", [I_cp(eci_ap, Er[l][:])], reads=[EK + "t"], writes=["eci"])
            S.op("dve", [I_cp(Ei[l][:], eci_ap)], reads=["eci"], writes=[EK + "r"])
            S.op("dve", [I_tt(Ei[l][:], Er[l][:], Ei[l][:], ALU.subtract)], reads=[EK + "t", EK + "r"], writes=[EK + "f"])
            S.op("act", I_act(Ei[l][:], Ei[l][:], AF.Sin, scale=TWO_PI), reads=[EK + "f"], writes=["Ei%d" % l])
            S.op("dve", [I_ts(Er[l][:], Er[l][:], 0.25, ALU.add)], reads=[EK + "t", EK + "f"], writes=[EK + "t2"])
            S.op("dve", [I_cp(eci_ap, Er[l][:])], reads=[EK + "t2", EK + "r"], writes=["eci2"])
            S.op("dve", [I_cp(ec0_ap, eci_ap)], reads=["eci2"], writes=["ec0b"])
            S.op("dve", [I_tt(Er[l][:], Er[l][:], ec0_ap, ALU.subtract)], reads=[EK + "t2", "ec0b"], writes=[EK + "f2"])
            S.op("act", I_act(Er[l][:], Er[l][:], AF.Sin, scale=TWO_PI), reads=[EK + "f2"], writes=["Er%d" % l])
            chk(1.2)
            den, nr, qr, qi, ta, tb = t16[4], t16[5], t16[6], t16[7], t16[8], t16[9]
            lr, li = sA[:, 0, :], sA[:, 1, :]
            S.op("dve", [I_tt(den[:], lr, lr, ALU.mult)], reads=["sA"], writes=["den0"])
            S.op("dve", [I_tt(ta[:], li, li, ALU.mult)], reads=["sA"], writes=["ta0"])
            S.op("dve", [I_tt(den[:], den[:], ta[:], ALU.add)], reads=["den0", "ta0"], writes=["den1"])
            S.op("dve", [I_recip(den[:], den[:])], reads=["den1"], writes=["den"])
            S.op("dve", [I_ts(nr[:], PWr[:, :, 1], -1.0, ALU.add)], reads=["PWr"], writes=["nr"])
            S.op("dve", [I_tt(ta[:], nr[:], lr, ALU.mult)], reads=["nr", "sA", "den1"], writes=["ta1"])
            S.op("dve", [I_tt(tb[:], PWi[:, :, 1], li, ALU.mult)], reads=["PWi", "sA"], writes=["tb1"])
            S.op("dve", [I_tt(qr[:], ta[:], tb[:], ALU.add)], reads=["ta1", "tb1"], writes=["qr0"])
            S.op("dve", [I_tt(qr[:], qr[:], den[:], ALU.mult)], reads=["qr0", "den"], writes=["qr"])
            S.op("dve", [I_tt(ta[:], PWi[:, :, 1], lr, ALU.mult)], reads=["PWi", "sA", "qr0"], writes=["ta2"])
            S.op("dve", [I_tt(tb[:], nr[:], li, ALU.mult)], reads=["nr", "sA", "qr0"], writes=["tb2"])
            S.op("dve", [I_tt(qi[:], ta[:], tb[:], ALU.subtract)], reads=["ta2", "tb2"], writes=["qi0"])
            S.op("dve", [I_tt(qi[:], qi[:], den[:], ALU.mult)], reads=["qi0", "den"], writes=["qi"])
            chk(1.25)
            b16 = lambda t: t[:].unsqueeze(2).to_broadcast([128, 16, 16])
            Z1s, Z2s = Z1[:, :, 0:16], Z2[:, :, 0:16]
            S.op("dve", [I_tt(Z1s, sB[:, 0], b16(qr), ALU.mult)], reads=["sB", "qr"], writes=["Z1"])
            S.op("dve", [I_tt(Z2s, sB[:, 1], b16(qi), ALU.mult)], reads=["sB", "qi"], writes=["Z2"])
            S.op("dve", [I_tt(bbr[:], Z1s, Z2s, ALU.subtract)], reads=["Z1", "Z2"], writes=["bbr"])
            S.op("dve", [I_tt(Z1s, sB[:, 1], b16(qr), ALU.mult)], reads=["sB", "qr", "bbr"], writes=["Z1"])
            S.op("dve", [I_tt(Z2s, sB[:, 0], b16(qi), ALU.mult)], reads=["sB", "qi", "bbr"], writes=["Z2"])
            S.op("dve", [I_tt(bbi[:], Z1s, Z2s, ALU.add)], reads=["Z1", "Z2"], writes=["bbi"])
            chk(1.3)
            v4 = lambda t: t[:].rearrange("p g (k q) -> p g k q", k=8)
            vb = lambda ap: ap.unsqueeze(2).to_broadcast([128, 16, 8, 16])
            pb_ = lambda ap: ap.unsqueeze(3).to_broadcast([128, 16, 8, 16])

            def cmul(outr, outi, vr, vi, pr, pi, kr, ki, tag, neg_i=False):
                S.op("dve", [I_tt(v4(Z1), vb(vr), pb_(pr), ALU.mult)], reads=kr, writes=["Z1"])
                S.op("dve", [I_tt(v4(Z2), vb(vi), pb_(pi), ALU.mult)], reads=ki, writes=["Z2"])
                S.op("dve", [I_tt(outr[:], Z1[:], Z2[:], ALU.subtract)], reads=["Z1", "Z2"], writes=[tag + "r"])
                S.op("dve", [I_tt(v4(Z1), vb(vr), pb_(pi), ALU.mult)], reads=kr + [tag + "r"], writes=["Z1"])
                S.op("dve", [I_tt(v4(Z2), vb(vi), pb_(pr), ALU.mult)], reads=ki + [tag + "r"], writes=["Z2"])
                S.op("dve", [I_tt(outi[:], Z1[:], Z2[:], ALU.add)], reads=["Z1", "Z2"], writes=[tag + "i"])

            cmul(BTr, BTi, bbr[:], bbi[:], PWr[:, :, 11:19], PWi[:, :, 11:19], ["bbr", "PWr", "PWi"], ["bbi", "PWr", "PWi"], "BT")
            cmul(Ccr, Cci, sC[:, 0], sC[:, 1], PWr[:, :, 1:9], PWi[:, :, 1:9], ["sC", "PWr", "PWi"], ["sC", "PWr", "PWi"], "Cc")
            chk(1.4)
            S.op("act", [I_acp(Cout[l][:, :, 0, :], Ccr[:])], reads=["Ccr"], writes=["Cout%dr" % l])
            S.op("dve", [I_ts(Cout[l][:, :, 1, :], Cci[:], -1.0, ALU.mult)], reads=["Cci"], writes=["Cout%di" % l])
            chk(1.45)
            m8r = PWr[:, :, 9:10].to_broadcast([128, 16, 128])
            m8i = PWi[:, :, 9:10].to_broadcast([128, 16, 128])
            S.op("dve", [I_tt(Z1[:], Ccr[:], m8r, ALU.mult)], reads=["Ccr", "PWr", "BTi"], writes=["Z1"])
            S.op("dve", [I_tt(Z2[:], Cci[:], m8i, ALU.mult)], reads=["Cci", "PWi", "BTi"], writes=["Z2"])
            S.op("dve", [I_tt(RCr[:], Z1[:], Z2[:], ALU.subtract)], reads=["Z1", "Z2"], writes=["RCr"])
            S.op("dve", [I_tt(Z1[:], Ccr[:], m8i, ALU.mult)], reads=["Ccr", "PWi", "RCr", "Cout%dr" % l], writes=["Z1"])
            S.op("dve", [I_tt(Z2[:], Cci[:], m8r, ALU.mult)], reads=["Cci", "PWr", "RCr", "Cout%di" % l], writes=["Z2"])
            chk(1.47)
            S.op("dve", [I_stt(RCn[:], Z1[:], -1.0, Z2[:], ALU.mult, ALU.subtract)], reads=["Z1", "Z2"], writes=["RCn", "Ccr"])
            chk(1.5)
            S.op("dve", [I_memset(RCbr[0:64, :, 1, :], 0.0), I_memset(RCbr[64:128, :, 0, :], 0.0)], reads=[], writes=["RCbr", "Z1"])
            S.op("dve", [I_memset(RCbn[0:64, :, 1, :], 0.0), I_memset(RCbn[64:128, :, 0, :], 0.0)], reads=[], writes=["RCbn", "Z2"])
            S.op("act", [I_acp(RCbr[0:64, :, 0, :], RCr[0:64]), I_acp(RCbr[64:128, :, 1, :], RCr[64:128])], reads=["RCr", "RCbr"], writes=["RCbr2"])
            S.op("dve", [I_cp(RCbn[0:64, :, 0, :], RCn[0:64]), I_cp(RCbn[64:128, :, 1, :], RCn[64:128])], reads=["RCn", "RCbn"], writes=["RCbn2"])
            S.op("act", [I_acp(BTrb[:], BTr[:])], reads=["BTr", "Z2"], writes=["BTrb", "Cci"])
            S.op("act", [I_acp(BTib[:], BTi[:])], reads=["BTi", "Z2"], writes=["BTib", "Cci"])
            for gp0 in range(0, 16, 2):
                pt, pkey = pbank()
                fns = []
                for a in range(2):
                    gp = gp0 + a
                    o = pt[:, a * 256:(a + 1) * 256]
                    fns.append(I_mm(o, BTrb[:, gp, :], RCbr[:, gp, :, :].rearrange("p a c -> p (a c)"), True, False))
                    fns.append(I_mm(o, BTib[:, gp, :], RCbn[:, gp, :, :].rearrange("p a c -> p (a c)"), False, True))
                S.op("pe", fns, reads=["BTrb", "BTib", "RCbr2", "RCbn2", "RCbr", "RCbn", "Z1", "Z2", "Cci", "Ccr"], writes=[pkey])
                g0 = 2 * gp0
                S.op("dve", [I_tt(Toep[l][:, g0:g0 + 4, :], pt[:].rearrange("p (g c) -> p g c", g=4),
                                  maskt[:].unsqueeze(1).to_broadcast([128, 4, 128]), ALU.mult)],
                     reads=[pkey, "maskt"], writes=["Toep%d_%d" % (l, g0)])
            chk(1.6)
            for ri_, BTx in enumerate((BTr, BTi)):
                for gp0 in range(0, 16, 4):
                    pt, pkey = pbank()
                    fns = [I_tr(pt[:, i * 128:(i + 1) * 128], BTx[:, gp0 + i, :], ident[:]) for i in range(4)]
                    S.op("pe", fns, reads=["BTr", "BTi", "ident"], writes=[pkey])
                    dst = Bin[l][:, 2 * gp0:2 * gp0 + 8, ri_, :]
                    S.op("act", [I_acp(dst, pt[:].rearrange("p (g n) -> p g n", g=8))], reads=[pkey],
                         writes=["Bin%d_%d_%d" % (l, ri_, gp0)])
        allk = ["sA", "sB", "sC", "kk", "cc", "dt", "lrd", "lid", "lid0", "pk0", "pk0e", "pk1", "pki", "pk2", "pk2f", "sin", "pk4",
                "pki2", "pk5", "pk5f", "cos", "PWr", "PWi", "eci", "eci2", "ec0b", "E0t", "E0r", "E0f", "E0t2", "E0f2", "E1t", "E1r", "E1f", "E1t2", "E1f2",
                "den0", "ta0", "den1", "den", "nr", "ta1", "tb1", "qr0", "qr", "ta2", "tb2", "qi0", "qi", "Z1", "Z2",
                "bbr", "bbi", "BTr", "BTi", "Ccr", "Cci", "RCr", "RCn", "RCbr", "RCbn", "RCbr2", "RCbn2", "BTrb", "BTib"]
        fence(allk + ["hstg0", "hstg1"])

    def halves(nt):
        h = nt // 2
        return [(0, h), (h, nt - h)]

    def load_x(b, nt):
        t0 = b * TBP
        for m in range(KT):
            S.dma("sp", xT[:, m, 0:nt], x_tT[m * 128:(m + 1) * 128, t0:t0 + nt], writes=["xT%d" % m], key="xload")

    def norm_stats(nt, src=None, skey="xT%d"):
        src = xT if src is None else src
        hv_ = halves(nt)
        for hi, (c0, n) in enumerate(hv_):
            sl = slice(c0, c0 + n)
            for m in range(KT):
                if m % 4 != 3:
                    S.op("act", I_act(sq[:, m, sl], src[:, m, sl], AF.Square), reads=[skey % m, ZONE], writes=["sq%d_%d" % (m, hi)])
                else:
                    S.op("dve", [I_tt(sq[:, m, sl], src[:, m, sl], src[:, m, sl], ALU.mult)], reads=[skey % m, ZONE], writes=["sq%d_%d" % (m, hi)])
        pts = []
        for hi, (c0, n) in enumerate(hv_):
            sl = slice(c0, c0 + n)
            pt, pkey = pbank()
            S.op("pe", [I_mm(pt[:, 0:n], ones[:], sq[:, m, sl], m == 0, m == KT - 1) for m in range(KT)],
                 reads=["ones", ZONE] + ["sq%d_%d" % (m, hi) for m in range(KT)], writes=[pkey])
            pts.append((pt, pkey))
        for hi, (c0, n) in enumerate(hv_):
            sl = slice(c0, c0 + n)
            pt, pkey = pts[hi]
            S.op("act", I_act(rt[:, sl], pt[:, 0:n], AF.Sqrt, scale=1.0 / D, bias=epsb[:]), reads=[pkey, "epsb", ZONE], writes=["rt%d" % hi])
        return hv_

    def rmsnorm(gcol, nt):
        hv_ = norm_stats(nt)
        for hi, (c0, n) in enumerate(hv_):
            sl = slice(c0, c0 + n)
            S.op("dve", [I_recip(rstd[:, sl], rt[:, sl])], reads=["rt%d" % hi, ZONE], writes=["rstd%d" % hi])
            for m in range(KT):
                S.op("dve", [I_stt(hT[:, m, sl], xT[:, m, sl], V(gcol + m), rstd[:, sl], ALU.mult, ALU.mult)],
                     reads=["xT%d" % m, "rstd%d" % hi, "vecs"], writes=["hT%d_h%d" % (m, hi)])

    def hT_keys():
        return ["hT%d_h%d" % (m, hi) for m in range(KT) for hi in range(2)]

    def w_in_loads(l, c0):
        src = w_in[l].rearrange("(kt p) c -> p kt c", p=128)[:, :, c0:c0 + 256]
        return [(lambda t: t[:], src)]

    def layer(b, l, nt):
        last = (b == NBLK - 1)
        cb = CH + (NSM if last else 0)
        hv = halves(nt)
        rmsnorm(V_N1 + l * 8, nt)
        chk(4)
        for cg in (2, 3, 0, 1):
            wt, wk = W.get("W", w_in_loads(l, cg * 256))
            for ci in range(2):
                col = cg * 2 + ci
                for hi, (c0, n) in enumerate(hv):
                    pt, pkey = pbank()
                    if cg == 2 and ci == 0:
                        for k in range(KT):
                            S.op("pe", [I_mm(pt[:, 0:n], wt[:, k, ci * 128:(ci + 1) * 128], hT[:, k, c0:c0 + n], k == 0, k == KT - 1)],
                                 reads=[wk, "hT%d_h%d" % (k, hi)], writes=[pkey])
                    else:
                        S.op("pe", [I_mm(pt[:, 0:n], wt[:, k, ci * 128:(ci + 1) * 128], hT[:, k, c0:c0 + n], k == 0, k == KT - 1)
                                    for k in range(KT)], reads=[wk] + hT_keys(), writes=[pkey])
                    if col >= 4:
                        tile_ = col - 4
                        npr = n if not (last and hi == 1) else n - NSM
                        cch0 = c0 // 8
                        nch = npr // 8
                        fns = [I_acp(usT[:, tile_, :, cch0:cch0 + nch].rearrange("p s c -> p c s"),
                                     pt[:, 0:npr].rearrange("p (c s) -> p c s", s=8))]
                        if last and hi == 1:
                            fns.append(I_acp(usT[:, tile_, 0, CH:CH + NSM], pt[:, npr:n]))
                        S.op("act", fns, reads=[pkey, ZONE], writes=["usT_%d" % tile_])
                        if hi == 1:
                            ssm_shuffle_in_write(tile_, cb, last)

                    else:
                        S.op("dve", [I_cp(upT[:, col, 16 + c0:16 + c0 + n], pt[:, 0:n])], reads=[pkey, ZONE], writes=["upT%d" % col])
            W.prefetch("W")
        chk(5)
        ssm_core(b, l, cb, last)
        chk(6)
        pool_branch(b, l, nt, last)
        ssm_out(b, l, nt, cb, last)
        chk(7)
        merge_loop(b, l, nt)
        for mp in range(0, KT, 2):
            src = w_out[l].rearrange("(kt p) c -> p kt c", p=128)[:, :, mp * 128:mp * 128 + 256]
            wt, wk = W.get("W", [(lambda t: t[:], src)])
            for mi in range(2):
                m = mp + mi
                for hi, (c0, n) in enumerate(hv):
                    pt, pkey = pbank()
                    S.op("pe", [I_mm(pt[:, 0:n], wt[:, k, mi * 128:(mi + 1) * 128], mg[:, k, c0:c0 + n], k == 0, k == KT - 1)
                                for k in range(KT)], reads=[wk, "mg", ZONE] + ["usT_%d" % t_ for t_ in range(4)] + ["usT_%dz" % t_ for t_ in range(4)] + ["U8_%d" % t_ for t_ in range(4)], writes=[pkey])
                    S.op("dve", [I_tt(xT[:, m, c0:c0 + n], xT[:, m, c0:c0 + n], pt[:, 0:n], ALU.add)],
                         reads=[pkey, "xT%d" % m], writes=["xT%d" % m])
            W.prefetch("W")
        chk(8)
        fence()
        rmsnorm(V_N2 + l * 8, nt)
        ffn(b, l, nt, last)
        fence()

    def ssm_shuffle_in_write(t_, cb, last):
        uk = "usT_%d" % t_
        if last:
            S.op("dve", [I_memset(usT[:, t_, 1:8, CH:CH + NSM], 0.0)], reads=[ZONE], writes=[uk + "z"])
        S.dma("sp", scr1[t_ * 128:(t_ + 1) * 128, :, 0:cb], usT[:, t_, :, 0:cb], reads=[uk, uk + "z", ZONE], writes=["scr1_%d" % t_])
        v = scr1.rearrange("(g p) s c -> p g s c", p=16)
        for s_ in range(8):
            if s_ == 0:
                S.dma("sp", U8[16 * s_:16 * s_ + 16, 8 * t_:8 * t_ + 8, 0:cb], v[:, 8 * t_:8 * t_ + 8, s_, 0:cb],
                      reads=["scr1_%d" % t_, ZONE], writes=["U8_%d" % t_], key="U8_%d" % t_)
            else:
                S.dma_more("sp", U8[16 * s_:16 * s_ + 16, 8 * t_:8 * t_ + 8, 0:cb], v[:, 8 * t_:8 * t_ + 8, s_, 0:cb], "U8_%d" % t_)

    def ssm_shuffle_in_read(cb):
        v = scr1.rearrange("(g p) s c -> p g s c", p=16)
        for s_ in range(8):
            if s_ == 0:
                S.dma("sp", U8[16 * s_:16 * s_ + 16, :, 0:cb], v[:, :, s_, 0:cb],
                      reads=["scr1_%d" % t_ for t_ in range(4)] + [ZONE], writes=["U8_%d" % t_ for t_ in range(4)], key="U8_0")
            else:
                S.dma_more("sp", U8[16 * s_:16 * s_ + 16, :, 0:cb], v[:, :, s_, 0:cb], "U8_0")

    def ssm_core(b, l, cb, last):
        sm = sml[l]
        SKa, SKb = "sml%da" % l, "sml%db" % l
        R8, e1r, e1i = sm[:, 0, :], sm[:, 1, :], sm[:, 2, :]
        Hr, Hi_ = Hend[:, l, 0, :], Hend[:, l, 1, :]
        za, zb = tini[:, 0, :], tini[:, 1, :]
        S.op("dve", [I_tt(za, Hr, e1r, ALU.mult)], reads=["Hend%d" % l, SKa], writes=["tiniA"])
        S.op("dve", [I_tt(zb, Hi_, e1i, ALU.mult)], reads=["Hend%d" % l, SKa], writes=["tiniB"])
        S.op("dve", [I_tt(hinit[:, 0, :], za, zb, ALU.subtract)], reads=["tiniA", "tiniB"], writes=["hinit0"])
        S.op("dve", [I_tt(za, Hr, e1i, ALU.mult)], reads=["Hend%d" % l, SKa, "hinit0"], writes=["tiniA"])
        S.op("dve", [I_tt(zb, Hi_, e1r, ALU.mult)], reads=["Hend%d" % l, SKa, "hinit0"], writes=["tiniB"])
        S.op("dve", [I_tt(hinit[:, 1, :], za, zb, ALU.add)], reads=["tiniA", "tiniB"], writes=["hinit1"])
        S.op("act", [I_acp(Rt[:], R8.unsqueeze(2).to_broadcast([128, 16, CH]))],
             reads=[SKa, ZONE] + ["Gt%d%s" % (q, x) for q in range(0, 16, 2) for x in ("r", "i", "s")], writes=["Rt"])
        S.op("act", [I_memzero(Rt[:, :, 0:1])], reads=[], writes=["Rt"])
        S.op("dve", [I_tt(hiR[:], hinit[:], R8.unsqueeze(1).to_broadcast([128, 2, 16]), ALU.mult)], reads=["hinit0", "hinit1", SKa], writes=["hiR"])
        S.op("act", [I_memzero(Hp[:])], reads=[ZONE] + hT_keys(), writes=["HpZ", "ysg0", "ysg1"])
        S.op("dve", [I_cp(Hp[0:64, 0, :, :, 0], Hend[0:64, l, :, :]), I_cp(Hp[64:128, 1, :, :, 0], Hend[64:128, l, :, :])],
             reads=["Hend%d" % l, "HpZ", ZONE], writes=["Hp0"])
        for gp0 in range(0, 16, 2):
            pt, pkey = pbank()
            pv = pt[:, 0:4 * cb].rearrange("p (a r c) -> p a r c", a=2, r=2)
            fns = []
            for a in range(2):
                gp = gp0 + a
                for par in range(2):
                    g = 2 * gp + par
                    for ri_ in range(2):
                        fns.append(I_mm(pv[64 * par:64 * par + 64, a, ri_, :], Bin[l][:, g, ri_, :], U8[:, g, 0:cb], True, True))
            S.op("pe", fns, reads=["U8_%d" % (gp0 // 4), ZONE] + ["Bin%d_%d_%d" % (l, r, q) for r in range(2) for q in range(0, 16, 4)], writes=[pkey])
            Xr, Xi = pv[:, :, 0, 0:CH], pv[:, :, 1, 0:CH]
            er, ei = Er[l][:, gp0:gp0 + 2, :], Ei[l][:, gp0:gp0 + 2, :]
            gq = gp0 % 8
            t1, t2 = T1[:, gq:gq + 2, :], T2[:, gq:gq + 2, :]
            gk = "Gt%d" % gp0
            S.op("dve", [I_tt(t1, Xr, er, ALU.mult)], reads=[pkey, "Er%d" % l, ZONE], writes=["T1_%d" % gq])
            S.op("dve", [I_tt(t2, Xi, ei, ALU.mult)], reads=[pkey, "Ei%d" % l, ZONE], writes=["T2_%d" % gq])
            S.op("dve", [I_tt(Gt[:, 0, gp0:gp0 + 2, 0:CH], t1, t2, ALU.add)], reads=["T1_%d" % gq, "T2_%d" % gq, ZONE], writes=[gk + "r"])
            S.op("dve", [I_tt(t1, Xi, er, ALU.mult)], reads=[pkey, "Er%d" % l, gk + "r"], writes=["T1_%d" % gq])
            S.op("dve", [I_tt(t2, Xr, ei, ALU.mult)], reads=[pkey, "Ei%d" % l, gk + "r"], writes=["T2_%d" % gq])
            S.op("dve", [I_tt(Gt[:, 1, gp0:gp0 + 2, 0:CH], t1, t2, ALU.subtract)], reads=["T1_%d" % gq, "T2_%d" % gq], writes=[gk + "i"])
            if last:
                S.op("dve", [I_cp(X0s[:, 0, gp0:gp0 + 2, :], pv[:, :, 0, CH:cb]), I_cp(X0s[:, 1, gp0:gp0 + 2, :], pv[:, :, 1, CH:cb])],
                     reads=[pkey, ZONE], writes=[gk + "s"])
        gk_ri = ["Gt%d%s" % (q, x) for q in range(0, 16, 2) for x in "ri"]
        S.op("dve", [I_tt(Gt[:, :, :, 0], Gt[:, :, :, 0], hiR[:], ALU.add)], reads=gk_ri + ["hiR"], writes=gk_ri)
        for ri_ in range(2):
            flat = Gt[:, ri_].rearrange("p g c -> p (g c)")
            S.op("dve", [I_scan(flat, Rt[:].rearrange("p g c -> p (g c)"), flat, 0.0)], reads=gk_ri + ["Rt"], writes=gk_ri)
        gkeys = ["Gt%d%s" % (q, x) for q in range(0, 16, 2) for x in "ri"]
        t1k = ["T1_%d" % q for q in range(0, 8, 2)]
        t2k = ["T2_%d" % q for q in range(0, 8, 2)]
        for h2 in range(2):
            gsl = slice(8 * h2, 8 * h2 + 8)
            Gr, Gi = Gt[:, 0, gsl, 0:CH], Gt[:, 1, gsl, 0:CH]
            er, ei = Er[l][:, gsl, :], Ei[l][:, gsl, :]
            hk = "_%d" % h2
            S.op("dve", [I_tt(T1[:], Gr, er, ALU.mult)], reads=gkeys + ["Er%d" % l], writes=t1k)
            S.op("dve", [I_tt(T2[:], Gi, ei, ALU.mult)], reads=gkeys + ["Ei%d" % l], writes=t2k)
            S.op("dve", [I_tt(Hp[0:64, 0, 0, gsl, 1:CH], T1[0:64, :, 0:CH - 1], T2[0:64, :, 0:CH - 1], ALU.subtract),
                         I_tt(Hp[64:128, 1, 0, gsl, 1:CH], T1[64:128, :, 0:CH - 1], T2[64:128, :, 0:CH - 1], ALU.subtract)],
                 reads=t1k + t2k + [ZONE, "HpZ"], writes=["Hp1" + hk])
            S.op("dve", [I_tt(Hend[:, l, 0, gsl], T1[:, :, CH - 1], T2[:, :, CH - 1], ALU.subtract)],
                 reads=t1k + t2k + ["Hp0", "hinit0", "hinit1", "tiniA", "tiniB"], writes=["Hend%d" % l])
            S.op("dve", [I_tt(T1[:], Gr, ei, ALU.mult)], reads=gkeys + ["Ei%d" % l, "Hp1" + hk, "Hend%d" % l], writes=t1k)
            S.op("dve", [I_tt(T2[:], Gi, er, ALU.mult)], reads=gkeys + ["Er%d" % l, "Hp1" + hk, "Hend%d" % l], writes=t2k)
            S.op("dve", [I_tt(Hp[0:64, 0, 1, gsl, 1:CH], T1[0:64, :, 0:CH - 1], T2[0:64, :, 0:CH - 1], ALU.add),
                         I_tt(Hp[64:128, 1, 1, gsl, 1:CH], T1[64:128, :, 0:CH - 1], T2[64:128, :, 0:CH - 1], ALU.add)],
                 reads=t1k + t2k + ["HpZ"], writes=["Hp2" + hk])
            S.op("dve", [I_tt(Hend[:, l, 1, gsl], T1[:, :, CH - 1], T2[:, :, CH - 1], ALU.add)], reads=t1k + t2k, writes=["Hend%d" % l])
        if last:
            S.dma("sp", o_hp[:, l], Hend[:, l], reads=["Hend%d" % l], writes=["o_hp%d" % l], key="outs", out_final=True)
            S.dma("sp", h0s[:], sth_d[:, l], reads=[ZONE], writes=["h0s", "pl_a"])
            S.op("dve", [I_cp(Hp[0:64, 0, :, :, CH:cb], h0s[0:64]), I_cp(Hp[64:128, 1, :, :, CH:cb], h0s[64:128])],
                 reads=["h0s", "HpZ", ZONE], writes=["Hp3"])
            a1r, a1i, m7r, m7i = sm[:, 3, :], sm[:, 4, :], sm[:, 5, :], sm[:, 6, :]
            bb_ = lambda ap: ap.unsqueeze(2).to_broadcast([128, 16, 16])
            X0r, X0i = X0s[:, 0], X0s[:, 1]
            gsk = ["Gt%ds" % q for q in range(0, 16, 2)]
            t = [hs_t[:, i] for i in range(4)]
            S.op("dve", [I_tt(t[0], h0s[:, 0], bb_(a1r), ALU.mult), I_tt(t[1], h0s[:, 1], bb_(a1i), ALU.mult),
                         I_tt(t[2], X0r, bb_(m7r), ALU.mult), I_tt(t[3], X0i, bb_(m7i), ALU.mult)],
                 reads=["h0s", SKb, ZONE] + gsk, writes=["pl_b"])
            S.op("dve", [I_tt(t[0], t[0], t[1], ALU.subtract), I_tt(t[2], t[2], t[3], ALU.subtract)], reads=["pl_b"], writes=["pl_b"])
            S.op("dve", [I_tt(hs_new[:, 0], t[0], t[2], ALU.add)], reads=["pl_b", ZONE], writes=["hs_newr", "pl_a"])
            S.op("dve", [I_tt(t[0], h0s[:, 0], bb_(a1i), ALU.mult), I_tt(t[1], h0s[:, 1], bb_(a1r), ALU.mult),
                         I_tt(t[2], X0r, bb_(m7i), ALU.mult), I_tt(t[3], X0i, bb_(m7r), ALU.mult)],
                 reads=["h0s", SKb, "hs_newr"] + gsk, writes=["pl_b"])
            S.op("dve", [I_tt(t[0], t[0], t[1], ALU.add), I_tt(t[2], t[2], t[3], ALU.add)], reads=["pl_b"], writes=["pl_b"])
            S.op("dve", [I_tt(hs_new[:, 1], t[0], t[2], ALU.add)], reads=["pl_b"], writes=["hs_newi", "pl_a"])
            S.dma("sp", o_hs[:, l], hs_new[:], reads=["hs_newr", "hs_newi", "pl_a", ZONE], writes=["o_hs%d" % l], key="outs", out_final=True)

    def ssm_out(b, l, nt, cb, last):
        gkeys = ["Gt%d%s" % (q, x) for q in range(0, 16, 2) for x in ("r", "i", "s")]
        v2 = scr2.rearrange("(g q) j c -> q g j c", q=16)

        def wr(t_):
            for j in range(8):
                if j == 0:
                    S.dma("sp", v2[:, 8 * t_:8 * t_ + 8, j, 0:cb], Y8[16 * j:16 * j + 16, 8 * t_:8 * t_ + 8, 0:cb],
                          reads=["Y8_%d" % (8 * t_), "Y8_%d" % (8 * t_ + 4), ZONE] + gkeys, writes=["scr2_%d" % t_], key="scr2_%d" % t_)
                else:
                    S.dma_more("sp", v2[:, 8 * t_:8 * t_ + 8, j, 0:cb], Y8[16 * j:16 * j + 16, 8 * t_:8 * t_ + 8, 0:cb], "scr2_%d" % t_)

        def rd(t_):
            S.dma("sp", ysT[:, t_, :, 0:cb], scr2[t_ * 128:(t_ + 1) * 128, :, 0:cb], reads=["scr2_%d" % t_, ZONE] + gkeys, writes=["ysT_%d" % t_])

        for g0 in range(0, 32, 4):
            h2 = (g0 // 2) // 8
            hpk = ["HpZ", "Hp0", "Hp1_%d" % h2, "Hp2_%d" % h2] + (["Hp3"] if last else [])
            pt, pkey = pbank()
            fns = []
            for gg in range(4):
                g = g0 + gg
                gp, par = g // 2, g % 2
                o = pt[:, gg * 128:gg * 128 + cb]
                fns.append(I_mm(o, Toep[l][:, g, :], U8[:, g, 0:cb], True, False))
                fns.append(I_mm(o, Cout[l][:, gp, 0, :], Hp[:, par, 0, gp, 0:cb], False, False))
                fns.append(I_mm(o, Cout[l][:, gp, 1, :], Hp[:, par, 1, gp, 0:cb], False, True))
            S.op("pe", fns, reads=["U8_%d" % (g0 // 8), ZONE, "Toep%d_%d" % (l, g0), "Cout%dr" % l, "Cout%di" % l] + hpk, writes=[pkey])
            S.op("act", [I_acp(Y8[:, g0:g0 + 4, 0:cb], pt[:].rearrange("p (g c) -> p g c", g=4)[:, :, 0:cb])],
                 reads=[pkey, ZONE], writes=["Y8_%d" % g0, "Rt"] + gkeys)
            if g0 == 28:
                y8all = ["Y8_%d" % q for q in range(0, 32, 4)]
                for j in range(8):
                    if j == 0:
                        S.dma("sp", v2[:, :, j, 0:cb], Y8[16 * j:16 * j + 16, :, 0:cb], reads=y8all + [ZONE] + gkeys, writes=["scr2_0"], key="scr2_0")
                    else:
                        S.dma_more("sp", v2[:, :, j, 0:cb], Y8[16 * j:16 * j + 16, :, 0:cb], "scr2_0")
                for t2 in range(4):
                    S.dma("sp", ysT[:, t2, :, 0:cb], scr2[t2 * 128:(t2 + 1) * 128, :, 0:cb], reads=["scr2_0", ZONE] + gkeys, writes=["ysT_%d" % t2])
        for tl in range(4):
            S.op("dve", [I_stt(ysT[:, tl, :, 0:cb], usT[:, tl, :, 0:cb], V(V_SSD + l * 4 + tl), ysT[:, tl, :, 0:cb], ALU.mult, ALU.add)],
                 reads=["ysT_%d" % tl, "usT_%d" % tl, "usT_%dz" % tl, "vecs", ZONE] + gkeys, writes=["ysT%d" % tl])
            fns = [I_act(gy[:, tl, 0:TBP].rearrange("p (c s) -> p c s", s=8), ysT[:, tl, :, 0:CH].rearrange("p s c -> p c s"),
                         AF.Gelu_apprx_tanh)]
            if last:
                fns.append(I_act(gy[:, tl, TBP:TBP + NSM], ysT[:, tl, 0, CH:cb], AF.Gelu_apprx_tanh))
            S.op("act", fns, reads=["ysT%d" % tl, ZONE] + gkeys, writes=["gy%d" % tl])

    def pool_branch(b, l, nt, last):
        npr = TBP
        for g in range(4):
            w = 2 << g
            uk = "upT%d" % g
            ext = upT[:, g, :]
            S.op("dve", [I_cp(ext[:, 0:16], uhist[:, l, g, :])], reads=["uhist%d_%d" % (l, g), ZONE], writes=[uk + "h"])
            cur_ = ext
            n_steps = g + 1
            bufs = [cbf[0], cbf[1]] if False else None
            sc_a = sg[0][:].rearrange("p a n -> p (a n)")
            sc_b = sg[1][:].rearrange("p a n -> p (a n)")
            src = ext
            lo = 0
            for st in range(n_steps):
                sh = 1 << st
                dst = sc_a if st % 2 == 0 else sc_b
                S.op("dve", [I_tt(dst[:, lo + sh:16 + npr], src[:, lo + sh:16 + npr], src[:, lo:16 + npr - sh], ALU.add)],
                     reads=[uk, uk + "h", "pl_a", "pl_b", ZONE], writes=["pl_a" if st % 2 == 0 else "pl_b"])
                src = dst
                lo += sh
            S.op("dve", [I_stt(dT[:, g, 0:npr], src[:, 16:16 + npr], 1.0 / w, ext[:, 16:16 + npr], ALU.mult, ALU.subtract)],
                 reads=["pl_a", "pl_b", uk, ZONE], writes=["dT%d" % g])
            if b == 0:
                free_ = sc_a if (n_steps % 2 == 0) else sc_b
                fk = "pl_a" if (n_steps % 2 == 0) else "pl_b"
                S.op("dve", [I_tt(free_[:, 0:16], src[:, 16:32], invc[:, g, :], ALU.mult)],
                     reads=["pl_a", "pl_b", "invc", "dT%d" % g, ZONE], writes=[fk])
                S.op("dve", [I_tt(dT[:, g, 0:16], free_[:, 0:16], ext[:, 16:32], ALU.subtract)],
                     reads=[fk, uk, ZONE], writes=["dT%d" % g])
            S.op("dve", [I_cp(uhist[:, l, g, :], ext[:, npr:npr + 16])], reads=[uk, uk + "h"], writes=["uhist%d_%d" % (l, g)])
            if last:
                S.dma("sp", o_pp[:, l, g, :], uhist[:, l, g, 1:16], reads=["uhist%d_%d" % (l, g)], writes=["o_pp%d_%d" % (l, g)], key="outs", out_final=True)
        if last:
            S.dma("sp", hsT[:], stp_d[:, l], reads=[ZONE], writes=["hsT"])
            for g in range(4):
                w = 2 << g
                us = upT[:, g, 16 + TBP:16 + TBP + NSM]
                S.op("dve", [I_red(hsum[:, g, :], hsT[:, g, :, 15 - (w - 1):15])], reads=["hsT", ZONE], writes=["hsum%d" % g])
                S.op("dve", [I_tt(hsum[:, g, :], hsum[:, g, :], us, ALU.add)], reads=["hsum%d" % g, "upT%d" % g], writes=["hsum%db" % g])
                S.op("dve", [I_stt(dT[:, g, TBP:TBP + NSM], hsum[:, g, :], 1.0 / w, us, ALU.mult, ALU.subtract)],
                     reads=["hsum%db" % g, "upT%d" % g], writes=["dT%d" % g])
                S.op("dve", [I_cp(usn[:, g, :], us)], reads=["upT%d" % g, ZONE], writes=["usn%d" % g])
                S.dma("sp", o_psn[:, l, g, :], usn[:, g, :], reads=["usn%d" % g, ZONE], writes=["o_psn%d_%d" % (l, g)], key="outs", out_final=True)

    def merge_loop(b, l, nt):
        hv = halves(nt)
        srcp = pool_w[l].rearrange("g p c -> p g c")
        wpt, wpk = W.get("P", [(lambda t: t[:], srcp)])
        for m in range(KT):
            wt, wk = W.get("W", [(lambda t: t[:, :, 0:128], w_in[l].rearrange("(kt p) c -> p kt c", p=128)[:, :, 1024 + m * 128:1024 + (m + 1) * 128]),
                                 (lambda t: t[:, :, 128:256], w_in[l].rearrange("(kt p) c -> p kt c", p=128)[:, :, 2048 + m * 128:2048 + (m + 1) * 128])])
            gt, gk = W.get("G", [(lambda t: t[:, :, 0:128], w_glu[l].rearrange("(kt p) c -> p kt c", p=128)[:, :, m * 128:(m + 1) * 128]),
                                 (lambda t: t[:, :, 128:256], w_glu[l].rearrange("(kt p) c -> p kt c", p=128)[:, :, 1024 + m * 128:1024 + (m + 1) * 128])])
            allu = ["usT_%d" % t_ for t_ in range(4)] + ["usT_%dz" % t_ for t_ in range(4)] + ["U8_%d" % t_ for t_ in range(4)]
            for hi, (c0, n) in enumerate(hv):
                sl = slice(c0, c0 + n)
                p_gp, k_gp = pbank()
                p_gs, k_gs = pbank()
                p_yp, k_yp = pbank()
                S.op("pe", [I_mm(p_gp[:, 0:n], wt[:, k, 0:128], hT[:, k, sl], k == 0, k == KT - 1) for k in range(KT)],
                     reads=[wk] + hT_keys(), writes=[k_gp])
                S.op("pe", [I_mm(p_gs[:, 0:n], wt[:, k, 128:256], hT[:, k, sl], k == 0, k == KT - 1) for k in range(KT)],
                     reads=[wk] + hT_keys(), writes=[k_gs])
                S.op("pe", [I_mm(p_yp[:, 0:n], wpt[:, m // 2, (m % 2) * 128:(m % 2) * 128 + 128], dT[:, m // 2, sl], True, True)],
                     reads=[wpk, ZONE, "dT%d" % (m // 2)], writes=[k_yp])
                sgt = sg[hi]
                S.op("act", I_act(sgt[:, 0, 0:n], p_gp[:, 0:n], AF.Sigmoid), reads=[k_gp, ZONE], writes=["sg%d_0" % hi, "pl_a" if hi == 0 else "pl_b"])
                S.op("act", I_act(sgt[:, 1, 0:n], p_gs[:, 0:n], AF.Sigmoid), reads=[k_gs, ZONE], writes=["sg%d_1" % hi, "pl_a" if hi == 0 else "pl_b"])
                S.op("dve", [I_stt(sgt[:, 0, 0:n], p_yp[:, 0:n], V(V_PSC + l * 8 + m), sgt[:, 0, 0:n], ALU.mult, ALU.mult)],
                     reads=[k_yp, "sg%d_0" % hi, "vecs", ZONE], writes=["sg%d_0" % hi])
            for hi, (c0, n) in enumerate(hv):
                sl = slice(c0, c0 + n)
                p_z1, k_z1 = pbank()
                p_z2, k_z2 = pbank()
                S.op("pe", [I_mm(p_z1[:, 0:n], gt[:, k, 0:128], gy[:, k, sl], k == 0, k == 3) for k in range(4)],
                     reads=[gk, ZONE] + ["gy%d" % k for k in range(4)], writes=[k_z1])
                S.op("pe", [I_mm(p_z2[:, 0:n], gt[:, k, 128:256], gy[:, k, sl], k == 0, k == 3) for k in range(4)],
                     reads=[gk, ZONE] + ["gy%d" % k for k in range(4)], writes=[k_z2])
                sgt = sg[hi]
                S.op("act", I_act(sgt[:, 2, 0:n], p_z2[:, 0:n], AF.Sigmoid), reads=[k_z2, ZONE], writes=["sg%d_2" % hi, "pl_a" if hi == 0 else "pl_b"])
                S.op("dve", [I_tt(sgt[:, 2, 0:n], p_z1[:, 0:n], sgt[:, 2, 0:n], ALU.mult)], reads=[k_z1, "sg%d_2" % hi, ZONE], writes=["sg%d_2" % hi])
                S.op("dve", [I_tt(sgt[:, 2, 0:n], sgt[:, 2, 0:n], sgt[:, 1, 0:n], ALU.mult)], reads=["sg%d_2" % hi, "sg%d_1" % hi], writes=["sg%d_2" % hi])
                S.op("dve", [I_tt(mg[:, m, sl], sgt[:, 0, 0:n], sgt[:, 2, 0:n], ALU.add)],
                     reads=["sg%d_0" % hi, "sg%d_2" % hi, ZONE], writes=["mg"] + allu)
            W.prefetch("W")
            W.prefetch("G")
        W.prefetch("P")

    def ffn(b, l, nt, last):
        hv = halves(nt)
        npr = TBP
        if last:
            S.dma("sp", stcs[:], stc_d[:, l], reads=[ZONE, "mg"], writes=["stcs"])
        for j in range(FT):
            wt, wk = W.get("W", [(lambda t: t[:, :, 0:128], w_up[l].rearrange("(kt p) c -> p kt c", p=128)[:, :, j * 128:(j + 1) * 128]),
                                 (lambda t: t[:, :, 128:256], w_up[l].rearrange("(kt p) c -> p kt c", p=128)[:, :, DFF + j * 128:DFF + (j + 1) * 128])])
            gs_, gsk = gS[j % 2], "gS%d" % (j % 2)
            cb_, cbk = cbf[j % 2], "cbf%d" % (j % 2)
            gc_, gck = gcb[j % 2], "gcb%d" % (j % 2)
            vs_, vsk = vS[j % 2], "vS%d" % (j % 2)
            for hi, (c0, n) in enumerate(hv):
                sl = slice(c0, c0 + n)
                p_g, k_g = pbank()
                p_v, k_v = pbank()
                if j == 0:
                    for k in range(KT):
                        S.op("pe", [I_mm(p_g[:, 0:n], wt[:, k, 0:128], hT[:, k, sl], k == 0, k == KT - 1)], reads=[wk, "hT%d_h%d" % (k, hi)], writes=[k_g])
                else:
                    S.op("pe", [I_mm(p_g[:, 0:n], wt[:, k, 0:128], hT[:, k, sl], k == 0, k == KT - 1) for k in range(KT)],
                         reads=[wk] + hT_keys(), writes=[k_g])
                S.op("pe", [I_mm(p_v[:, 0:n], wt[:, k, 128:256], hT[:, k, sl], k == 0, k == KT - 1) for k in range(KT)],
                     reads=[wk] + hT_keys(), writes=[k_v])
                S.op("act", [I_acp(gs_[:, 2 + c0:2 + c0 + n], p_g[:, 0:n])], reads=[k_g, ZONE], writes=[gsk + "_%d" % hi])
                S.op("act", [I_acp(vs_[:, c0:c0 + n], p_v[:, 0:n])], reads=[k_v, ZONE], writes=[vsk + "_%d" % hi])
            W.prefetch("W")
            gk2 = [gsk + "_0", gsk + "_1"]
            S.op("dve", [I_cp(gs_[:, 0:2], ghist[:, l, j, :])], reads=["ghist%d_%d" % (l, j), ZONE], writes=[gsk + "_h"])
            allg = gk2 + [gsk + "_h"]
            cw = lambda k: V(V_CW + (l * 3 + k) * FT + j)
            cbias = V(V_CB + l * FT + j)
            S.op("act", I_act(cb_[:, 0:npr], gs_[:, 0:npr], AF.Identity, scale=cw(0), bias=cbias), reads=allg + ["vecs", ZONE], writes=[cbk])
            S.op("dve", [I_stt(cb_[:, 0:npr], gs_[:, 1:1 + npr], cw(1), cb_[:, 0:npr], ALU.mult, ALU.add)], reads=allg + [cbk, "vecs"], writes=[cbk + "b"])
            S.op("dve", [I_stt(cb_[:, 0:npr], gs_[:, 2:2 + npr], cw(2), cb_[:, 0:npr], ALU.mult, ALU.add)], reads=allg + [cbk + "b"], writes=[cbk + "c"])
            ckeys = [cbk + "c"]
            if last:
                cs = cb_[:, npr:npr + NSM]
                S.op("dve", [I_ts(cs, stcs[:, j, :, 0], cw(0), ALU.mult, cbias, ALU.add)], reads=["stcs", "vecs", ZONE, cbk + "c"], writes=[cbk + "s"])
                S.op("dve", [I_stt(cs, stcs[:, j, :, 1], cw(1), cs, ALU.mult, ALU.add)], reads=["stcs", cbk + "s"], writes=[cbk + "s2"])
                S.op("dve", [I_stt(cs, gs_[:, 2 + npr:2 + npr + NSM], cw(2), cs, ALU.mult, ALU.add)], reads=allg + [cbk + "s2"], writes=[cbk + "s3"])
                ckeys.append(cbk + "s3")
                S.dma("sp", o_csn[:, l, j, :], gs_[:, 2 + npr:2 + npr + NSM], reads=allg + [ZONE], writes=["o_csn%d_%d" % (l, j)], key="outs", out_final=True)
            S.op("dve", [I_cp(ghist[:, l, j, :], gs_[:, npr:npr + 2])], reads=allg, writes=["ghist%d_%d" % (l, j)])
            if last:
                S.dma("sp", o_cp[:, l, j, :], ghist[:, l, j, :], reads=["ghist%d_%d" % (l, j)], writes=["o_cp%d_%d" % (l, j)], key="outs", out_final=True)
            S.op("act", I_act(gc_[:, 0:nt], cb_[:, 0:nt], AF.Gelu_apprx_tanh), reads=ckeys + [ZONE], writes=[gck])
            S.op("dve", [I_tt(a_ff[:, j, 0:nt], gc_[:, 0:nt], vs_[:, 0:nt], ALU.mult)], reads=[gck, vsk + "_0", vsk + "_1", ZONE], writes=["a_ff%d" % j])
        for m in range(KT):
            src = w_down[l].rearrange("(kt p) c -> p kt c", p=128)[:, :, m * 128:(m + 1) * 128]
            wt, wk = W.get("D", [(lambda t: t[:], src)])
            for hi, (c0, n) in enumerate(hv):
                pt, pkey = pbank()
                if m == 0:
                    S.op("pe", [I_mm(pt[:, 0:n], wt[:, k, :], a_ff[:, k, c0:c0 + n], k == 0, False) for k in range(FT - 2)],
                         reads=[wk, ZONE] + ["a_ff%d" % k for k in range(FT - 2)], writes=[pkey])
                    S.op("pe", [I_mm(pt[:, 0:n], wt[:, k, :], a_ff[:, k, c0:c0 + n], False, k == FT - 1) for k in range(FT - 2, FT)],
                         reads=[wk, ZONE] + ["a_ff%d" % k for k in range(FT - 2, FT)], writes=[pkey])
                else:
                    S.op("pe", [I_mm(pt[:, 0:n], wt[:, k, :], a_ff[:, k, c0:c0 + n], k == 0, k == FT - 1) for k in range(FT)],
                         reads=[wk, ZONE] + ["a_ff%d" % k for k in range(FT)], writes=[pkey])
                S.op("dve", [I_tt(xT[:, m, c0:c0 + n], xT[:, m, c0:c0 + n], pt[:, 0:n], ALU.add)],
                     reads=[pkey, "xT%d" % m], writes=["xT%d" % m])
            W.prefetch("D")

    def final_out(b, nt):
        t0 = b * TBP
        nb = b + 1
        for m in range(KT):
            if m % 2:
                S.op("act", [I_acp(yT[:, m, 0:nt], xT[:, m, 0:nt])], reads=["xT%d" % m, ZONE], writes=["yTm%d" % m])
            else:
                S.op("dve", [I_cp(yT[:, m, 0:nt], xT[:, m, 0:nt])], reads=["xT%d" % m, ZONE], writes=["yTm%d" % m])
            if nb < NBLK:
                nt2 = TBP + (NSM if nb == NBLK - 1 else 0)
                S.dma("sp", xT[:, m, 0:nt2], x_tT[m * 128:(m + 1) * 128, nb * TBP:nb * TBP + nt2], writes=["xT%d" % m], key="xload")
        hv_ = norm_stats(nt, src=yT, skey="yTm%d")
        for hi, (c0, n) in enumerate(hv_):
            S.op("dve", [I_recip(rstd[:, c0:c0 + n], rt[:, c0:c0 + n])], reads=["rt%d" % hi, ZONE], writes=["rstd%d" % hi])
        for m in range(KT):
            st, sk = ysg[m % 2], "ysg%d" % (m % 2)
            S.op("dve", [I_stt(st[:, c0:c0 + n], yT[:, m, c0:c0 + n], V(V_NF + m), rstd[:, c0:c0 + n], ALU.mult, ALU.mult) for (c0, n) in hv_],
                 reads=["yTm%d" % m, "rstd0", "rstd1", "vecs", "HpZ", ZONE], writes=[sk])
            S.dma("sp", y_allT[m * 128:(m + 1) * 128, t0:t0 + nt], st[:, 0:nt], reads=[sk], writes=["y_all"], key="outs", out_final=True)
        fence()

    S.limit = limit
    S.dry = True
    S.calls = 0
    try:
        gen()
    except StopGen:
        pass
    S.dry = False
    S.calls = 0
    try:
        gen()
    except StopGen:
        pass
    S.emit()
    es.close()
    return nc


def _host_layouts(inp):
    f32 = np.float32
    c = {}
    def fm(v, ntile):
        v = np.asarray(v, f32)
        lead = v.shape[:-1]
        return np.ascontiguousarray(np.moveaxis(v.reshape(lead + (ntile, 128)), -1, 0))
    vecs = np.zeros((128, V_END), f32)
    vecs[:, V_N1:V_N1 + 16] = fm(inp["norm1_g"], 8).reshape(128, 16)
    vecs[:, V_N2:V_N2 + 16] = fm(inp["norm2_g"], 8).reshape(128, 16)
    vecs[:, V_NF:V_NF + 8] = fm(inp["norm_f_g"], 8).reshape(128, 8)
    vecs[:, V_PSC:V_PSC + 16] = fm(inp["pool_scale"], 8).reshape(128, 16)
    vecs[:, V_SSD:V_SSD + 8] = fm(inp["ssm_D"], 4).reshape(128, 8)
    vecs[:, V_CW:V_CW + 132] = fm(inp["conv_w"], FT).reshape(128, 132)
    vecs[:, V_CB:V_CB + 44] = fm(inp["conv_b"], FT).reshape(128, 44)
    c["vecs"] = vecs
    def gl(a):
        a = np.asarray(a, f32)
        rest = a.shape[3:]
        a = a.reshape((L, 16, 2, 64) + rest)
        a = np.moveaxis(a, (2, 3), (0, 1))
        return np.ascontiguousarray(a.reshape((128, L, 16) + rest))
    A_re, A_im = gl(inp["ssm_A_re"]), gl(inp["ssm_A_im"])
    ldt = gl(np.broadcast_to(np.asarray(inp["ssm_log_dt"], f32)[:, :, None], (L, 32, 64)))
    c["ssmA"] = np.ascontiguousarray(np.stack([A_re, A_im, ldt], axis=2))
    c["ssmB"] = np.ascontiguousarray(np.stack([gl(inp["ssm_B_re"]), gl(inp["ssm_B_im"])], axis=2))
    Ct = lambda a: gl(np.swapaxes(np.asarray(a, f32), 2, 3))
    c["ssmC"] = np.ascontiguousarray(np.stack([Ct(inp["ssm_C_re"]), Ct(inp["ssm_C_im"])], axis=2))
    c["ident"] = np.eye(128, dtype=f32)
    s_idx = np.arange(128) // 16
    c["mask"] = (s_idx[None, :] >= s_idx[:, None]).astype(f32)
    c["kvec"] = np.ascontiguousarray(np.broadcast_to(np.asarray(KLIST, f32)[None, None, :], (128, 16, NK)))
    c["cvec"] = np.ascontiguousarray(np.broadcast_to(np.arange(CH, dtype=f32)[None, None, :], (128, 16, CH)))
    invc = np.zeros((128, 4, 16), f32)
    for g in range(4):
        invc[:, g, :] = 1.0 / np.minimum(2 << g, np.arange(16) + 1)
    c["invc"] = invc
    return c


_PROG = None


def make_in_maps(inp, cores=range(NCORES)):
    f32 = np.float32
    common = _host_layouts(inp)
    for k in ("w_in", "pool_w", "w_glu", "w_out", "w_up", "w_down"):
        common[k] = np.ascontiguousarray(np.asarray(inp[k], f32))
    xp = np.asarray(inp["x_prompt"], f32)
    xs = np.asarray(inp["x_sample"], f32)
    meta = np.asarray(inp["meta_tokens"], f32)
    st_pool = np.asarray(inp["state_pool"], f32)
    st_re = np.asarray(inp["state_ssm_re"], f32)
    st_im = np.asarray(inp["state_ssm_im"], f32)
    st_conv = np.asarray(inp["state_conv"], f32)
    in_maps = []
    for i in cores:
        m = dict(common)
        bs = slice(NSM * i, NSM * (i + 1))
        m["x_tT"] = np.ascontiguousarray(np.concatenate([meta, xp[i], xs[bs, 0, :]], axis=0).T)
        sp = st_pool[:, bs]
        m["sp_raw"] = np.ascontiguousarray(sp)
        m["stp"] = np.ascontiguousarray(sp.reshape(L, NSM, 15, 4, 128).transpose(4, 0, 3, 1, 2))
        sc = st_conv[:, bs]
        m["sc_raw"] = np.ascontiguousarray(sc)
        m["stc"] = np.ascontiguousarray(sc.reshape(L, NSM, 2, FT, 128).transpose(4, 0, 3, 1, 2))
        h = np.stack([st_re[:, bs], st_im[:, bs]], axis=0)
        h = h.reshape(2, L, NSM, 16, 2, 64).transpose(4, 5, 1, 0, 3, 2)
        m["sth"] = np.ascontiguousarray(h.reshape(128, L, 2, 16, NSM))
        in_maps.append(m)
    return in_maps


def assemble_core(r):
    f32 = np.float32
    o = {}
    y_all = np.ascontiguousarray(r["y_allT"].T)
    o["y_prompt"] = y_all[NMETA:NPR]
    o["y_sample"] = y_all[NPR:NTOK][:, None, :]
    o["pool_p"] = r["o_pp"].transpose(1, 3, 2, 0).reshape(L, 15, 512)
    a = r["o_hp"].reshape(2, 64, L, 2, 16).transpose(2, 3, 4, 0, 1).reshape(L, 2, 32, 64)
    o["re_p"], o["im_p"] = a[:, 0], a[:, 1]
    o["conv_p"] = r["o_cp"].transpose(1, 3, 2, 0).reshape(L, 2, DFF)
    new_row = r["o_psn"].transpose(1, 3, 2, 0).reshape(L, NSM, 1, 512)
    o["pool_s"] = np.concatenate([r["o_psh"][:, :, 1:15, :], new_row], axis=2)
    a = r["o_hs"].reshape(2, 64, L, 2, 16, NSM).transpose(2, 3, 5, 4, 0, 1).reshape(L, 2, NSM, 32, 64)
    o["re_s"], o["im_s"] = a[:, 0], a[:, 1]
    newc = r["o_csn"].transpose(1, 3, 2, 0).reshape(L, NSM, 1, DFF)
    o["conv_s"] = np.concatenate([r["o_csh"][:, :, 1:2, :], newc], axis=2)
    return o


def kernel(**inp):
    global _PROG
    f32 = np.float32
    if _PROG is None:
        _PROG = build_program()
    nc = _PROG
    in_maps = make_in_maps(inp)
    res = run_bass_kernel_spmd(nc, in_maps, core_ids=list(range(NCORES)))
    P = [assemble_core(r) for r in res.results]
    y_prompt = np.stack([p["y_prompt"] for p in P], axis=0)
    y_sample = np.concatenate([p["y_sample"] for p in P], axis=0)
    stk = lambda k: np.stack([p[k] for p in P], axis=1)
    cat = lambda k: np.concatenate([p[k] for p in P], axis=1)
    outs = (y_prompt, y_sample, stk("pool_p"), stk("re_p"), stk("im_p"), stk("conv_p"),
            cat("pool_s"), cat("re_s"), cat("im_s"), cat("conv_s"))
    return tuple(np.ascontiguousarray(o, dtype=f32) for o in outs)
```

```python
import numpy as np
from contextlib import ExitStack
import concourse.bass as bass
import concourse.mybir as mybir
from concourse.bass_utils import run_bass_kernel_spmd

F32 = mybir.dt.float32
BF16 = mybir.dt.bfloat16
I32 = mybir.dt.int32
ALU = mybir.AluOpType
AF = mybir.ActivationFunctionType
AX = mybir.AxisListType

ENGS = ["pe", "act", "dve", "pool", "sp"]
NCORES = 8
D = 1024
KT = 8
L = 2
DFF = 2816
FT = 22
NMETA = 16
SEQ = 2048
NPR = NMETA + SEQ
NSM = 16
NTOK = NPR + NSM
NBLK = 3
CH = 86
TBP = CH * 8
TWO_PI = float(2.0 * np.pi)
NK = 19
KLIST = [0, 1, 2, 3, 4, 5, 6, 7, 8, -8, -7, 7, 6, 5, 4, 3, 2, 1, 0]
V_N1, V_N2, V_NF, V_PSC, V_SSD, V_CW, V_CB, V_END = 0, 16, 32, 40, 56, 64, 196, 240


class StopGen(Exception):
    pass


class Sched:
    def __init__(self, nc, es, n_dma_sems=44):
        self.nc = nc
        self.dry = False
        self.ops = {e: [] for e in ENGS}
        self.cnt = {e: 0 for e in ENGS}
        self.esem = {e: es.enter_context(nc.semaphore("s_" + e)) for e in ENGS}
        self.free_dma_sems = [es.enter_context(nc.semaphore("d%d" % i)) for i in range(n_dma_sems)]
        self.dma_sem = {}
        self.last_w = {}
        self.readers = {}
        self.waited = {e: {} for e in ENGS}
        self.out_events = []

    def _need(self, eng, ev, waits):
        if ev is None:
            return
        sem, val, src = ev
        if src == eng and eng == "pe":
            return
        if isinstance(src, tuple):
            val = max(val, self.dma_sem[src[1]][1])
        w = self.waited[eng]
        if w.get(sem.name, 0) >= val:
            return
        w[sem.name] = val
        waits.append((sem, val))

    def _deps(self, eng, reads, writes):
        waits = []
        for k in reads:
            self._need(eng, self.last_w.get(k), waits)
        for k in writes:
            self._need(eng, self.last_w.get(k), waits)
            for ev in self.readers.get(k, []):
                self._need(eng, ev, waits)
        return waits

    def _commit(self, ev, reads, writes):
        for k in reads:
            self.readers.setdefault(k, []).append(ev)
        for k in writes:
            self.last_w[k] = ev
            self.readers[k] = []

    calls = 0
    limit = None

    def _count(self):
        self.calls += 1
        if self.limit is not None and self.calls > self.limit:
            raise StopGen()

    def op(self, eng, fns, reads=(), writes=()):
        self._count()
        if self.dry:
            return None
        if callable(fns):
            fns = [fns]
        waits = self._deps(eng, reads, writes)
        self.cnt[eng] += 1
        ev = (self.esem[eng], self.cnt[eng], eng)
        self.ops[eng].append((waits, fns, (self.esem[eng], 1)))
        self._commit(ev, reads, writes)
        return ev

    def dma(self, eng, out, in_, reads=(), writes=(), key=None, out_final=False):
        self._count()
        if self.dry:
            return None
        if key is None:
            key = writes[0]
        if key not in self.dma_sem:
            self.dma_sem[key] = [self.free_dma_sems.pop(), 0]
        ent = self.dma_sem[key]
        waits = self._deps(eng, reads, writes)
        ent[1] += 16
        ev = (ent[0], ent[1], ("dma", key))
        self.ops[eng].append((waits, [lambda e: e.dma_start(out=out, in_=in_)], (ent[0], 16)))
        self._commit(ev, reads, writes)
        if out_final:
            self.out_events.append(ev)
        return ev

    def dma_more(self, eng, out, in_, key):
        self._count()
        if self.dry:
            return None
        ent = self.dma_sem[key]
        ent[1] += 16
        ev = (ent[0], ent[1], ("dma", key))
        self.ops[eng].append(([], [lambda e: e.dma_start(out=out, in_=in_)], (ent[0], 16)))
        for k, v in list(self.last_w.items()):
            if v[0] is ent[0] and isinstance(v[2], tuple) and v[2][1] == key:
                self.last_w[k] = ev
        return ev

    def emit(self):
        nc = self.nc
        finals = {}
        for ev in self.out_events:
            cur = finals.get(ev[0].name)
            if cur is None or cur[1] < ev[1]:
                finals[ev[0].name] = (ev[0], ev[1])
        ops = self.ops

        def run(eng_name):
            def f(e):
                for waits, fns, inc in ops[eng_name]:
                    for sem, val in waits:
                        e.wait_ge(sem, val)
                    ins = None
                    for fn in fns:
                        ins = fn(e)
                    ins.then_inc(inc[0], inc[1])
                if eng_name == "sp":
                    for sem, val in finals.values():
                        e.wait_ge(sem, val)
            return f

        with nc.Block() as block:
            block.sync(run("sp"))
            block.scalar(run("act"))
            block.vector(run("dve"))
            block.gpsimd(run("pool"))
            block.tensor(run("pe"))


def I_tt(out, a, b, op):
    return lambda e: e.tensor_tensor(out=out, in0=a, in1=b, op=op)


def I_ts(out, a, s1, op0, s2=None, op1=None):
    if op1 is None:
        return lambda e: e.tensor_scalar(out=out, in0=a, scalar1=s1, scalar2=None, op0=op0)
    return lambda e: e.tensor_scalar(out=out, in0=a, scalar1=s1, scalar2=s2, op0=op0, op1=op1)


def I_stt(out, a, s, b, op0, op1):
    return lambda e: e.scalar_tensor_tensor(out=out, in0=a, scalar=s, in1=b, op0=op0, op1=op1)


def I_act(out, a, func, **kw):
    return lambda e: e.activation(out=out, in_=a, func=func, **kw)


def I_cp(out, a):
    return lambda e: e.tensor_copy(out=out, in_=a)


def I_acp(out, a):
    return lambda e: e.copy(out=out, in_=a)


def I_mm(out, lhsT, rhs, start, stop):
    return lambda e: e.matmul(out, lhsT=lhsT, rhs=rhs, start=start, stop=stop)


def I_tr(out, in_, ident):
    return lambda e: e.transpose(out, in_, ident)


def I_memset(out, v):
    return lambda e: e.memset(out, v)


def I_memzero(out):
    return lambda e: e.memzero(out)


def I_scan(out, d0, d1, init):
    return lambda e: e.tensor_tensor_scan(out=out, data0=d0, data1=d1, initial=init, op0=ALU.mult, op1=ALU.add)


def I_red(out, a):
    return lambda e: e.tensor_reduce(out=out, in_=a, axis=AX.X, op=ALU.add)


def I_recip(out, a):
    return lambda e: e.reciprocal(out=out, in_=a)


class Alloc:
    def __init__(self, nc):
        self.nc = nc
        self.base = (nc.sbuf_base + 63) // 64 * 64
        self.top = nc.sbuf_top
        self.cur = self.base

    def _sz(self, shape, dt):
        n = 1
        for s in shape[1:]:
            n *= s
        b = 2 if dt == BF16 else 4
        return (n * b + 63) // 64 * 64

    def new(self, name, shape, dt):
        off = self.cur
        self.off = getattr(self, "off", {})
        self.off[name] = off
        self.cur += self._sz(shape, dt)
        assert self.cur <= self.top, ("SBUF overflow", name, self.cur, self.top)
        return self.nc.alloc_sbuf_tensor_at(name, shape, dt, offset=off)

    def at(self, name, shape, dt, off):
        assert off + self._sz(shape, dt) <= self.top, ("SBUF overflow", name, off, self._sz(shape, dt), self.top)
        return self.nc.alloc_sbuf_tensor_at(name, shape, dt, offset=off), off + self._sz(shape, dt)


def build_program(stage=99, limit=None):
    nc = bass.Bass("TRN2", target_bir_lowering=False)

    def chk(n):
        if stage < n:
            raise StopGen()

    dram_in = lambda n, s, dt=F32: nc.dram_tensor(n, list(s), dt, kind="ExternalInput").ap()
    dram_out = lambda n, s, dt=F32: nc.dram_tensor(n, list(s), dt, kind="ExternalOutput").ap()
    x_tT = dram_in("x_tT", [D, NTOK])
    w_in = dram_in("w_in", [L, D, 3072])
    pool_w = dram_in("pool_w", [L, 4, 128, 256])
    w_glu = dram_in("w_glu", [L, 512, 2048])
    w_out = dram_in("w_out", [L, D, D])
    w_up = dram_in("w_up", [L, D, 2 * DFF])
    w_down = dram_in("w_down", [L, DFF, D])
    vecs_d = dram_in("vecs", [128, V_END])
    ssmA_d = dram_in("ssmA", [128, L, 3, 16])
    ssmB_d = dram_in("ssmB", [128, L, 2, 16, 16])
    ssmC_d = dram_in("ssmC", [128, L, 2, 16, 16])
    stp_d = dram_in("stp", [128, L, 4, 16, 15])
    sth_d = dram_in("sth", [128, L, 2, 16, 16])
    stc_d = dram_in("stc", [128, L, FT, 16, 2])
    sp_raw = dram_in("sp_raw", [L, 16, 15, 512])
    sc_raw = dram_in("sc_raw", [L, 16, 2, DFF])
    ident_d = dram_in("ident", [128, 128])
    mask_d = dram_in("mask", [128, 128])
    kvec_d = dram_in("kvec", [128, 16, NK])
    cvec_d = dram_in("cvec", [128, 16, CH])
    invc_d = dram_in("invc", [128, 4, 16])

    y_allT = dram_out("y_allT", [D, NTOK])
    o_pp = dram_out("o_pp", [128, L, 4, 15])
    o_psn = dram_out("o_psn", [128, L, 4, 16])
    o_psh = dram_out("o_psh", [L, 16, 15, 512])
    o_hp = dram_out("o_hp", [128, L, 2, 16])
    o_hs = dram_out("o_hs", [128, L, 2, 16, 16])
    o_cp = dram_out("o_cp", [128, L, FT, 2])
    o_csn = dram_out("o_csn", [128, L, FT, 16])
    o_csh = dram_out("o_csh", [L, 16, 2, DFF])
    scr1 = nc.dram_tensor("scr1", [512, 8, 102], BF16, kind="Internal").ap()
    scr2 = nc.dram_tensor("scr2", [512, 8, 102], BF16, kind="Internal").ap()

    es = ExitStack()
    S = Sched(nc, es)
    A = Alloc(nc)
    NTM = TBP + NSM
    CB = CH + NSM

    xT = A.new("xT", [128, KT, NTM], F32)
    hT = A.new("hT", [128, KT, NTM], BF16)
    rstd = A.new("rstd", [128, NTM], F32)
    vecs = A.new("vecs", [128, V_END], F32)
    ident = A.new("ident", [128, 128], F32)
    maskt = A.new("maskt", [128, 128], F32)
    invc = A.new("invc", [128, 4, 16], F32)
    ones = A.new("ones", [128, 128], BF16)
    epsb = A.new("epsb", [128, 1], F32)
    Toep = [A.new("Toep%d" % l, [128, 32, 128], BF16) for l in range(L)]
    Bin = [A.new("Bin%d" % l, [128, 32, 2, 64], BF16) for l in range(L)]
    Cout = [A.new("Cout%d" % l, [128, 16, 2, 128], BF16) for l in range(L)]
    Er = [A.new("Er%d" % l, [128, 16, CH], F32) for l in range(L)]
    Ei = [A.new("Ei%d" % l, [128, 16, CH], F32) for l in range(L)]
    sml = [A.new("sml%d" % l, [128, 7, 16], F32) for l in range(L)]
    Hend = A.new("Hend", [128, L, 2, 16], F32)
    hinit = A.new("hinit", [128, 2, 16], F32)
    tini = A.new("tini", [128, 2, 16], F32)
    uhist = A.new("uhist", [128, L, 4, 16], F32)
    ghist = A.new("ghist", [128, L, FT, 2], F32)
    wW = [A.new("wW%d" % i, [128, KT, 256], BF16) for i in range(3)]
    zone = A.cur
    upT = A.new("upT", [128, 4, 16 + NTM], BF16)
    dT = A.new("dT", [128, 4, NTM], BF16)
    usT = A.new("usT", [128, 4, 8, CB], BF16)
    U8 = A.new("U8", [128, 32, CB], BF16)
    Gt = A.new("Gt", [128, 2, 16, CH], F32)
    Rt = A.new("Rt", [128, 16, CH], F32)
    X0s = A.new("X0s", [128, 2, 16, NSM], F32)
    hiR = A.new("hiR", [128, 2, 16], F32)
    T1 = A.new("T1", [128, 8, CH], F32)
    T2 = A.new("T2", [128, 8, CH], F32)
    Hp = A.new("Hp", [128, 2, 2, 16, CB], BF16)
    gy = A.new("gy", [128, 4, NTM], BF16)
    sg = [A.new("sg%d" % i, [128, 3, NTM // 2], F32) for i in range(2)]
    usn = A.new("usn", [128, 4, 16], F32)
    wG = [A.new("wG%d" % i, [128, 4, 256], BF16) for i in range(2)]
    wP = A.new("wP", [128, 4, 256], BF16)
    hsT = A.new("hsT", [128, 4, 16, 15], F32)
    hsum = A.new("hsum", [128, 4, 16], F32)
    mix_end = A.cur
    hs_t, _ = A.at("hs_t", [128, 4, 16, 16], F32, A.off["sg1"])
    h0s, _o = A.at("h0s", [128, 2, 16, 16], F32, A.off["sg0"])
    hs_new, _o = A.at("hs_new", [128, 2, 16, 16], F32, _o)
    assert _o <= A.off["sg0"] + A._sz([128, 3, NTM // 2], F32)
    ysg = []
    _o = A.off["Hp"]
    for i in range(2):
        t, _o = A.at("ysg%d" % i, [128, NTM], F32, _o)
        ysg.append(t)
    assert _o <= A.off["Hp"] + A._sz([128, 2, 2, 16, CB], BF16)
    gt_span = A._sz([128, 2, 16, CH], F32) + A._sz([128, 16, CH], F32)
    Y8, nxt = A.at("Y8", [128, 32, CB], BF16, A.off["Gt"])
    ysT, nxt = A.at("ysT", [128, 4, 8, CB], BF16, nxt)
    assert nxt <= A.off["Gt"] + gt_span
    sq, nxt2 = A.at("sq", [128, KT, NTM], BF16, A.off["Gt"])
    assert nxt2 <= A.off["Gt"] + gt_span
    rt, nxt2 = A.at("rt", [128, NTM], F32, A.off["T1"])
    assert nxt2 <= A.off["T1"] + 2 * A._sz([128, 8, CH], F32)
    mg, nxt = A.at("mg", [128, KT, NTM], BF16, A.off["usT"])
    assert nxt <= A.off["Gt"], (nxt, A.off["Gt"])
    cur = zone
    a_ff, cur = A.at("a_ff", [128, FT, NTM], BF16, cur)
    gS = []
    for i in range(2):
        t, cur = A.at("gS%d" % i, [128, 2 + NTM], F32, cur)
        gS.append(t)
    cbf = []
    for i in range(2):
        t, cur = A.at("cb%d" % i, [128, NTM], F32, cur)
        cbf.append(t)
    gcb = []
    for i in range(2):
        t, cur = A.at("gc%d" % i, [128, NTM], F32, cur)
        gcb.append(t)
    vS = []
    for i in range(2):
        t, cur = A.at("vS%d" % i, [128, NTM], F32, cur)
        vS.append(t)
    wD = []
    for i in range(2):
        t, cur = A.at("wD%d" % i, [128, FT, 256], BF16, cur)
        wD.append(t)
    stcs, cur = A.at("stcs", [128, FT, 16, 2], F32, cur)
    ffn_end = cur
    yT, cur2 = A.at("yT", [128, KT, NTM], F32, zone)
    ystg = []
    tok_in = []
    for i in range(2):
        t_, _ = A.at("tok_in%d" % i, [128, D], F32, cur2)
        tok_in.append(t_)
        t, cur2 = A.at("ystg%d" % i, [128, D], F32, cur2)
        ystg.append(t)
    hstg = []
    cur3 = zone
    zoff = {}

    def ztmp(name, shape, dt=F32):
        nonlocal cur3
        zoff[name] = cur3
        t, cur3 = A.at(name, shape, dt, cur3)
        return t
    sA = ztmp("sA", [128, 3, 16])
    kk = ztmp("kk", [128, 16, NK])
    cc = ztmp("cc", [128, 16, CH])
    t16 = [ztmp("t16_%d" % i, [128, 16]) for i in range(12)]
    pk = [ztmp("pk%d" % i, [128, 16, NK]) for i in range(6)]
    pki = ztmp("pki", [128, 16, NK], I32)
    PWr = ztmp("PWr", [128, 16, NK])
    PWi = ztmp("PWi", [128, 16, NK])
    sB = ztmp("sB", [128, 2, 16, 16])
    sC = ztmp("sC", [128, 2, 16, 16])
    bbr = ztmp("bbr", [128, 16, 16])
    bbi = ztmp("bbi", [128, 16, 16])
    BTr = ztmp("BTr", [128, 16, 128])
    BTi = ztmp("BTi", [128, 16, 128])
    Ccr = ztmp("Ccr", [128, 16, 128])
    Cci = ztmp("Cci", [128, 16, 128])
    RCr = ztmp("RCr", [128, 16, 128])
    RCn = Ccr
    Z1 = ztmp("Z1", [128, 16, 128])
    Z2 = ztmp("Z2", [128, 16, 128])
    RCbr, _ = A.at("RCbr", [128, 16, 2, 128], BF16, zoff["Z1"])
    RCbn, _ = A.at("RCbn", [128, 16, 2, 128], BF16, zoff["Z2"])
    BTrb, _o2 = A.at("BTrb", [128, 16, 128], BF16, zoff["Cci"])
    BTib, _o2 = A.at("BTib", [128, 16, 128], BF16, _o2)
    ec0_ap = Z1[:, :, 0:CH]
    eci_ap = Z2[:, :, 0:CH].bitcast(I32)
    for i in range(2):
        t, cur3 = A.at("hstg%d" % i, [128, 512], F32, cur3)
        hstg.append(t)
    assert max(mix_end, ffn_end, cur2, cur3) <= A.top, (mix_end, ffn_end, cur2, cur3, A.top)
    ZONE = "ZONE"

    banks = [nc.alloc_psum_tensor("pb%d" % i, [128, 512], F32) for i in range(8)]
    bank_i = [0]

    def pbank():
        i = bank_i[0] % 8
        bank_i[0] += 1
        return banks[i], "pb%d" % i

    phase = [0]
    ZPOOLS = ("G", "P", "D")

    class WS:
        def __init__(self):
            self.plan = {}
            self.pos = {}
            self.issued = {}
            self.slots = {"W": wW, "G": wG, "P": [wP], "D": wD}

        def get(self, pool, loads):
            if S.dry:
                self.plan.setdefault(pool, []).append((loads, phase[0]))
                return self.slots[pool][0], pool + "0"
            i = self.pos.get(pool, 0)
            self.pos[pool] = i + 1
            n = len(self.slots[pool])
            self._issue_upto(pool, i)
            slot = i % n
            return self.slots[pool][slot], "%s%d" % (pool, slot)

        def _issue_upto(self, pool, upto):
            n = len(self.slots[pool])
            j = self.issued.get(pool, 0)
            while j <= upto and j < len(self.plan[pool]):
                loads, ph = self.plan[pool][j]
                if pool in ZPOOLS and ph != phase[0]:
                    break
                slot = j % n
                for li, (dst_fn, src) in enumerate(loads):
                    if li == 0:
                        S.dma("pool", dst_fn(self.slots[pool][slot]), src, reads=([ZONE] if pool in ZPOOLS else []),
                              writes=["%s%d" % (pool, slot)])
                    else:
                        S.dma_more("pool", dst_fn(self.slots[pool][slot]), src, "%s%d" % (pool, slot))
                j += 1
            self.issued[pool] = j

        def prefetch(self, pool):
            if S.dry:
                return
            n = len(self.slots[pool])
            i = self.pos.get(pool, 0)
            self._issue_upto(pool, i + n - 1)

        def new_phase(self):
            if S.dry:
                return
            for pool in ZPOOLS:
                if pool in self.plan:
                    n = len(self.slots[pool])
                    self._issue_upto(pool, self.pos.get(pool, 0) + n - 1)

    W = WS()

    V = lambda c0, n=1: vecs[:, c0:c0 + n]

    def fence(extra=()):
        S.op("dve", [lambda e: e.memset(T1[:, 0, 0:1], 0.0)], reads=[], writes=list(extra) + [ZONE])
        phase[0] += 1
        W.new_phase()

    def history_copies():
        spv = sp_raw.rearrange("l b r c -> l (b r) c")
        opv = o_psh.rearrange("l b r c -> l (b r) c")
        scv = sc_raw.rearrange("l b r (q c) -> l (b r q) c", c=256)
        ocv = o_csh.rearrange("l b r (q c) -> l (b r q) c", c=256)
        pieces = []
        for l in range(L):
            for r0 in (0, 120):
                pieces.append((spv[l, r0:r0 + 120, :], opv[l, r0:r0 + 120, :], 120, 512))
            for r0, n in ((0, 128), (128, 128), (256, 96)):
                pieces.append((scv[l, r0:r0 + n, :], ocv[l, r0:r0 + n, :], n, 256))
        for i, (src, dst, n, w) in enumerate(pieces):
            hb, hk = hstg[i % 2], "hstg%d" % (i % 2)
            S.dma("sp", hb[0:n, 0:w], src, writes=[hk])
            S.dma("sp", dst, hb[0:n, 0:w], reads=[hk], writes=["o_hist"], key="outs", out_final=True)

    def gen():
        bank_i[0] = 0
        phase[0] = 0
        S.dma("sp", vecs[:], vecs_d, writes=["vecs"])
        chk(0)
        S.dma("sp", ident[:], ident_d, writes=["ident"])
        S.dma("sp", maskt[:], mask_d, writes=["maskt"])
        S.dma("sp", invc[:], invc_d, writes=["invc"])
        chk(0.2)
        S.op("dve", [I_memset(ones[:], 1.0), I_memset(epsb[:], 1e-6), I_memset(Hend[:], 0.0),
                     I_memset(uhist[:], 0.0), I_memset(ghist[:], 0.0), I_memset(hinit[:], 0.0)],
             writes=["ones", "epsb", "Hend", "uhist", "ghist", "hinit"])
        chk(0.4)
        chk(0.6)
        chk(1)
        load_x(0, TBP)
        setup_ssm()
        chk(2)
        for b in range(NBLK):
            nt = TBP + (NSM if b == NBLK - 1 else 0)
            chk(3)
            for l in range(L):
                layer(b, l, nt)
                chk(9 if b == 0 else 10 + b - 0.5 + 0.1 * l)
            final_out(b, nt)
            chk(10 + b)

    def setup_ssm():
        S.dma("sp", kk[:], kvec_d, writes=["kk"])
        S.dma("sp", cc[:], cvec_d, writes=["cc"])
        for l in range(L):
            S.dma("sp", sA[:], ssmA_d[:, l], writes=["sA"])
            S.dma("sp", sB[:], ssmB_d[:, l], writes=["sB"])
            S.dma("sp", sC[:], ssmC_d[:, l], writes=["sC"])
            if l == 0:
                history_copies()
            dt_, lrd, lid, r1 = t16[0], t16[1], t16[2], t16[3]
            S.op("act", I_act(dt_[:], sA[:, 2, :], AF.Exp), reads=["sA"], writes=["dt"])
            S.op("dve", [I_tt(lrd[:], sA[:, 0, :], dt_[:], ALU.mult)], reads=["sA", "dt"], writes=["lrd"])
            S.op("dve", [I_ts(lid[:], sA[:, 1, :], 1.0 / TWO_PI, ALU.mult)], reads=["sA"], writes=["lid0"])
            S.op("dve", [I_tt(lid[:], lid[:], dt_[:], ALU.mult)], reads=["lid0", "dt"], writes=["lid"])
            bc = lambda t: t[:].unsqueeze(2).to_broadcast([128, 16, NK])
            S.op("dve", [I_tt(pk[0][:], kk[:], bc(lrd), ALU.mult)], reads=["kk", "lrd"], writes=["pk0"])
            S.op("act", I_act(pk[0][:], pk[0][:], AF.Exp), reads=["pk0"], writes=["pk0e"])
            S.op("dve", [I_tt(pk[1][:], kk[:], bc(lid), ALU.mult)], reads=["kk", "lid"], writes=["pk1"])
            S.op("dve", [I_cp(pki[:], pk[1][:])], reads=["pk1"], writes=["pki"])
            S.op("dve", [I_cp(pk[2][:], pki[:])], reads=["pki"], writes=["pk2"])
            S.op("dve", [I_tt(pk[2][:], pk[1][:], pk[2][:], ALU.subtract)], reads=["pk1", "pk2"], writes=["pk2f"])
            S.op("act", I_act(pk[3][:], pk[2][:], AF.Sin, scale=TWO_PI), reads=["pk2f"], writes=["sin"])
            S.op("dve", [I_ts(pk[4][:], pk[1][:], 0.25, ALU.add)], reads=["pk1"], writes=["pk4"])
            S.op("dve", [I_cp(pki[:], pk[4][:])], reads=["pk4", "pk2"], writes=["pki2"])
            S.op("dve", [I_cp(pk[5][:], pki[:])], reads=["pki2"], writes=["pk5"])
            S.op("dve", [I_tt(pk[5][:], pk[4][:], pk[5][:], ALU.subtract)], reads=["pk4", "pk5"], writes=["pk5f"])
            S.op("act", I_act(pk[4][:], pk[5][:], AF.Sin, scale=TWO_PI), reads=["pk5f"], writes=["cos"])
            S.op("dve", [I_tt(PWr[:], pk[0][:], pk[4][:], ALU.mult)], reads=["pk0e", "cos"], writes=["PWr"])
            S.op("dve", [I_tt(PWi[:], pk[0][:], pk[3][:], ALU.mult)], reads=["pk0e", "sin"], writes=["PWi"])
            chk(1.1)
            sm = sml[l]
            SK = "sml%d" % l
            S.op("dve", [I_cp(sm[:, 0, :], pk[0][:, :, 8]), I_cp(sm[:, 1, :], pk[4][:, :, 8]), I_cp(sm[:, 2, :], pk[3][:, :, 8])],
                 reads=["pk0e", "cos", "sin"], writes=[SK + "a"])
            S.op("dve", [I_cp(sm[:, 3, :], PWr[:, :, 1]), I_cp(sm[:, 4, :], PWi[:, :, 1]),
                         I_cp(sm[:, 5, :], PWr[:, :, 10]), I_cp(sm[:, 6, :], PWi[:, :, 10])],
                 reads=["PWr", "PWi"], writes=[SK + "b"])
            f8 = pk[2][:, :, 8]
            bce = lambda ap: ap.unsqueeze(2).to_broadcast([128, 16, CH])
            EK = "E%d" % l
            S.op("dve", [I_tt(Er[l][:], cc[:], bce(f8), ALU.mult)], reads=["cc", "pk2f"], writes=[EK + "t"])
            S.op("dve", [I_cp(eci_ap, Er[l][:])], reads=[EK + "t"], writes=["eci"])
            S.op("dve", [I_cp(Ei[l][:], eci_ap)], reads=["eci"], writes=[EK + "r"])
            S.op("dve", [I_tt(Ei[l][:], Er[l][:], Ei[l][:], ALU.subtract)], reads=[EK + "t", EK + "r"], writes=[EK + "f"])
            S.op("act", I_act(Ei[l][:], Ei[l][:], AF.Sin, scale=TWO_PI), reads=[EK + "f"], writes=["Ei%d" % l])
            S.op("dve", [I_ts(Er[l][:], Er[l][:], 0.25, ALU.add)], reads=[EK + "t", EK + "f"], writes=[EK + "t2"])
            S.op("dve", [I_cp(eci_ap, Er[l][:])], reads=[EK + "t2", EK + "r"], writes=["eci2"])
            S.op("dve", [I_cp(ec0_ap, eci_ap)], reads=["eci2"], writes=["ec0b"])
            S.op("dve", [I_tt(Er[l][:], Er[l][:], ec0_ap, ALU.subtract)], reads=[EK + "t2", "ec0b"], writes=[EK + "f2"])
            S.op("act", I_act(Er[l][:], Er[l][:], AF.Sin, scale=TWO_PI), reads=[EK + "f2"], writes=["Er%d" % l])
            chk(1.2)
            den, nr, qr, qi, ta, tb = t16[4], t16[5], t16[6], t16[7], t16[8], t16[9]
            lr, li = sA[:, 0, :], sA[:, 1, :]
            S.op("dve", [I_tt(den[:], lr, lr, ALU.mult)], reads=["sA"], writes=["den0"])
            S.op("dve", [I_tt(ta[:], li, li, ALU.mult)], reads=["sA"], writes=["ta0"])
            S.op("dve", [I_tt(den[:], den[:], ta[:], ALU.add)], reads=["den0", "ta0"], writes=["den1"])
            S.op("dve", [I_recip(den[:], den[:])], reads=["den1"], writes=["den"])
            S.op("dve", [I_ts(nr[:], PWr[:, :, 1], -1.0, ALU.add)], reads=["PWr"], writes=["nr"])
            S.op("dve", [I_tt(ta[:], nr[:], lr, ALU.mult)], reads=["nr", "sA", "den1"], writes=["ta1"])
            S.op("dve", [I_tt(tb[:], PWi[:, :, 1], li, ALU.mult)], reads=["PWi", "sA"], writes=["tb1"])
            S.op("dve", [I_tt(qr[:], ta[:], tb[:], ALU.add)], reads=["ta1", "tb1"], writes=["qr0"])
            S.op("dve", [I_tt(qr[:], qr[:], den[:], ALU.mult)], reads=["qr0", "den"], writes=["qr"])
            S.op("dve", [I_tt(ta[:], PWi[:, :, 1], lr, ALU.mult)], reads=["PWi", "sA", "qr0"], writes=["ta2"])
            S.op("dve", [I_tt(tb[:], nr[:], li, ALU.mult)], reads=["nr", "sA", "qr0"], writes=["tb2"])
            S.op("dve", [I_tt(qi[:], ta[:], tb[:], ALU.subtract)], reads=["ta2", "tb2"], writes=["qi0"])
            S.op("dve", [I_tt(qi[:], qi[:], den[:], ALU.mult)], reads=["qi0", "den"], writes=["qi"])
            chk(1.25)
            b16 = lambda t: t[:].unsqueeze(2).to_broadcast([128, 16, 16])
            Z1s, Z2s = Z1[:, :, 0:16], Z2[:, :, 0:16]
            S.op("dve", [I_tt(Z1s, sB[:, 0], b16(qr), ALU.mult)], reads=["sB", "qr"], writes=["Z1"])
            S.op("dve", [I_tt(Z2s, sB[:, 1], b16(qi), ALU.mult)], reads=["sB", "qi"], writes=["Z2"])
            S.op("dve", [I_tt(bbr[:], Z1s, Z2s, ALU.subtract)], reads=["Z1", "Z2"], writes=["bbr"])
            S.op("dve", [I_tt(Z1s, sB[:, 1], b16(qr), ALU.mult)], reads=["sB", "qr", "bbr"], writes=["Z1"])
            S.op("dve", [I_tt(Z2s, sB[:, 0], b16(qi), ALU.mult)], reads=["sB", "qi", "bbr"], writes=["Z2"])
            S.op("dve", [I_tt(bbi[:], Z1s, Z2s, ALU.add)], reads=["Z1", "Z2"], writes=["bbi"])
            chk(1.3)
            v4 = lambda t: t[:].rearrange("p g (k q) -> p g k q", k=8)
            vb = lambda ap: ap.unsqueeze(2).to_broadcast([128, 16, 8, 16])
            pb_ = lambda ap: ap.unsqueeze(3).to_broadcast([128, 16, 8, 16])

            def cmul(outr, outi, vr, vi, pr, pi, kr, ki, tag, neg_i=False):
                S.op("dve", [I_tt(v4(Z1), vb(vr), pb_(pr), ALU.mult)], reads=kr, writes=["Z1"])
                S.op("dve", [I_tt(v4(Z2), vb(vi), pb_(pi), ALU.mult)], reads=ki, writes=["Z2"])
                S.op("dve", [I_tt(outr[:], Z1[:], Z2[:], ALU.subtract)], reads=["Z1", "Z2"], writes=[tag + "r"])
                S.op("dve", [I_tt(v4(Z1), vb(vr), pb_(pi), ALU.mult)], reads=kr + [tag + "r"], writes=["Z1"])
                S.op("dve", [I_tt(v4(Z2), vb(vi), pb_(pr), ALU.mult)], reads=ki + [tag + "r"], writes=["Z2"])
                S.op("dve", [I_tt(outi[:], Z1[:], Z2[:], ALU.add)], reads=["Z1", "Z2"], writes=[tag + "i"])

            cmul(BTr, BTi, bbr[:], bbi[:], PWr[:, :, 11:19], PWi[:, :, 11:19], ["bbr", "PWr", "PWi"], ["bbi", "PWr", "PWi"], "BT")
            cmul(Ccr, Cci, sC[:, 0], sC[:, 1], PWr[:, :, 1:9], PWi[:, :, 1:9], ["sC", "PWr", "PWi"], ["sC", "PWr", "PWi"], "Cc")
            chk(1.4)
            S.op("act", [I_acp(Cout[l][:, :, 0, :], Ccr[:])], reads=["Ccr"], writes=["Cout%dr" % l])
            S.op("dve", [I_ts(Cout[l][:, :, 1, :], Cci[:], -1.0, ALU.mult)], reads=["Cci"], writes=["Cout%di" % l])
            chk(1.45)
            m8r = PWr[:, :, 9:10].to_broadcast([128, 16, 128])
            m8i = PWi[:, :, 9:10].to_broadcast([128, 16, 128])
            S.op("dve", [I_tt(Z1[:], Ccr[:], m8r, ALU.mult)], reads=["Ccr", "PWr", "BTi"], writes=["Z1"])
            S.op("dve", [I_tt(Z2[:], Cci[:], m8i, ALU.mult)], reads=["Cci", "PWi", "BTi"], writes=["Z2"])
            S.op("dve", [I_tt(RCr[:], Z1[:], Z2[:], ALU.subtract)], reads=["Z1", "Z2"], writes=["RCr"])
            S.op("dve", [I_tt(Z1[:], Ccr[:], m8i, ALU.mult)], reads=["Ccr", "PWi", "RCr", "Cout%dr" % l], writes=["Z1"])
            S.op("dve", [I_tt(Z2[:], Cci[:], m8r, ALU.mult)], reads=["Cci", "PWr", "RCr", "Cout%di" % l], writes=["Z2"])
            chk(1.47)
            S.op("dve", [I_stt(RCn[:], Z1[:], -1.0, Z2[:], ALU.mult, ALU.subtract)], reads=["Z1", "Z2"], writes=["RCn", "Ccr"])
            chk(1.5)
            S.op("dve", [I_memset(RCbr[0:64, :, 1, :], 0.0), I_memset(RCbr[64:128, :, 0, :], 0.0)], reads=[], writes=["RCbr", "Z1"])
            S.op("dve", [I_memset(RCbn[0:64, :, 1, :], 0.0), I_memset(RCbn[64:128, :, 0, :], 0.0)], reads=[], writes=["RCbn", "Z2"])
            S.op("act", [I_acp(RCbr[0:64, :, 0, :], RCr[0:64]), I_acp(RCbr[64:128, :, 1, :], RCr[64:128])], reads=["RCr", "RCbr"], writes=["RCbr2"])
            S.op("dve", [I_cp(RCbn[0:64, :, 0, :], RCn[0:64]), I_cp(RCbn[64:128, :, 1, :], RCn[64:128])], reads=["RCn", "RCbn"], writes=["RCbn2"])
            S.op("act", [I_acp(BTrb[:], BTr[:])], reads=["BTr", "Z2"], writes=["BTrb", "Cci"])
            S.op("act", [I_acp(BTib[:], BTi[:])], reads=["BTi", "Z2"], writes=["BTib", "Cci"])
            for gp0 in range(0, 16, 2):
                pt, pkey = pbank()
                fns = []
                for a in range(2):
                    gp = gp0 + a
                    o = pt[:, a * 256:(a + 1) * 256]
                    fns.append(I_mm(o, BTrb[:, gp, :], RCbr[:, gp, :, :].rearrange("p a c -> p (a c)"), True, False))
                    fns.append(I_mm(o, BTib[:, gp, :], RCbn[:, gp, :, :].rearrange("p a c -> p (a c)"), False, True))
                S.op("pe", fns, reads=["BTrb", "BTib", "RCbr2", "RCbn2", "RCbr", "RCbn", "Z1", "Z2", "Cci", "Ccr"], writes=[pkey])
                g0 = 2 * gp0
                S.op("dve", [I_tt(Toep[l][:, g0:g0 + 4, :], pt[:].rearrange("p (g c) -> p g c", g=4),
                                  maskt[:].unsqueeze(1).to_broadcast([128, 4, 128]), ALU.mult)],
                     reads=[pkey, "maskt"], writes=["Toep%d_%d" % (l, g0)])
            chk(1.6)
            for ri_, BTx in enumerate((BTr, BTi)):
                for gp0 in range(0, 16, 4):
                    pt, pkey = pbank()
                    fns = [I_tr(pt[:, i * 128:(i + 1) * 128], BTx[:, gp0 + i, :], ident[:]) for i in range(4)]
                    S.op("pe", fns, reads=["BTr", "BTi", "ident"], writes=[pkey])
                    dst = Bin[l][:, 2 * gp0:2 * gp0 + 8, ri_, :]
                    S.op("act", [I_acp(dst, pt[:].rearrange("p (g n) -> p g n", g=8))], reads=[pkey],
                         writes=["Bin%d_%d_%d" % (l, ri_, gp0)])
        allk = ["sA", "sB", "sC", "kk", "cc", "dt", "lrd", "lid", "lid0", "pk0", "pk0e", "pk1", "pki", "pk2", "pk2f", "sin", "pk4",
                "pki2", "pk5", "pk5f", "cos", "PWr", "PWi", "eci", "eci2", "ec0b", "E0t", "E0r", "E0f", "E0t2", "E0f2", "E1t", "E1r", "E1f", "E1t2", "E1f2",
                "den0", "ta0", "den1", "den", "nr", "ta1", "tb1", "qr0", "qr", "ta2", "tb2", "qi0", "qi", "Z1", "Z2",
                "bbr", "bbi", "BTr", "BTi", "Ccr", "Cci", "RCr", "RCn", "RCbr", "RCbn", "RCbr2", "RCbn2", "BTrb", "BTib"]
        fence(allk + ["hstg0", "hstg1"])

    def halves(nt):
        h = nt // 2
        return [(0, h), (h, nt - h)]

    def load_x(b, nt):
        t0 = b * TBP
        for m in range(KT):
            S.dma("sp", xT[:, m, 0:nt], x_tT[m * 128:(m + 1) * 128, t0:t0 + nt], writes=["xT%d" % m], key="xload")

    def norm_stats(nt, src=None, skey="xT%d"):
        src = xT if src is None else src
        hv_ = halves(nt)
        for hi, (c0, n) in enumerate(hv_):
            sl = slice(c0, c0 + n)
            for m in range(KT):
                if m % 4 != 3:
                    S.op("act", I_act(sq[:, m, sl], src[:, m, sl], AF.Square), reads=[skey % m, ZONE], writes=["sq%d_%d" % (m, hi)])
                else:
                    S.op("dve", [I_tt(sq[:, m, sl], src[:, m, sl], src[:, m, sl], ALU.mult)], reads=[skey % m, ZONE], writes=["sq%d_%d" % (m, hi)])
        pts = []
        for hi, (c0, n) in enumerate(hv_):
            sl = slice(c0, c0 + n)
            pt, pkey = pbank()
            S.op("pe", [I_mm(pt[:, 0:n], ones[:], sq[:, m, sl], m == 0, m == KT - 1) for m in range(KT)],
                 reads=["ones", ZONE] + ["sq%d_%d" % (m, hi) for m in range(KT)], writes=[pkey])
            pts.append((pt, pkey))
        for hi, (c0, n) in enumerate(hv_):
            sl = slice(c0, c0 + n)
            pt, pkey = pts[hi]
            S.op("act", I_act(rt[:, sl], pt[:, 0:n], AF.Sqrt, scale=1.0 / D, bias=epsb[:]), reads=[pkey, "epsb", ZONE], writes=["rt%d" % hi])
        return hv_

    def rmsnorm(gcol, nt):
        hv_ = norm_stats(nt)
        for hi, (c0, n) in enumerate(hv_):
            sl = slice(c0, c0 + n)
            S.op("dve", [I_recip(rstd[:, sl], rt[:, sl])], reads=["rt%d" % hi, ZONE], writes=["rstd%d" % hi])
            for m in range(KT):
                S.op("dve", [I_stt(hT[:, m, sl], xT[:, m, sl], V(gcol + m), rstd[:, sl], ALU.mult, ALU.mult)],
                     reads=["xT%d" % m, "rstd%d" % hi, "vecs"], writes=["hT%d_h%d" % (m, hi)])

    def hT_keys():
        return ["hT%d_h%d" % (m, hi) for m in range(KT) for hi in range(2)]

    def w_in_loads(l, c0):
        src = w_in[l].rearrange("(kt p) c -> p kt c", p=128)[:, :, c0:c0 + 256]
        return [(lambda t: t[:], src)]

    def layer(b, l, nt):
        last = (b == NBLK - 1)
        cb = CH + (NSM if last else 0)
        hv = halves(nt)
        rmsnorm(V_N1 + l * 8, nt)
        chk(4)
        for cg in (2, 3, 0, 1):
            wt, wk = W.get("W", w_in_loads(l, cg * 256))
            for ci in range(2):
                col = cg * 2 + ci
                for hi, (c0, n) in enumerate(hv):
                    pt, pkey = pbank()
                    if cg == 2 and ci == 0:
                        for k in range(KT):
                            S.op("pe", [I_mm(pt[:, 0:n], wt[:, k, ci * 128:(ci + 1) * 128], hT[:, k, c0:c0 + n], k == 0, k == KT - 1)],
                                 reads=[wk, "hT%d_h%d" % (k, hi)], writes=[pkey])
                    else:
                        S.op("pe", [I_mm(pt[:, 0:n], wt[:, k, ci * 128:(ci + 1) * 128], hT[:, k, c0:c0 + n], k == 0, k == KT - 1)
                                    for k in range(KT)], reads=[wk] + hT_keys(), writes=[pkey])
                    if col >= 4:
                        tile_ = col - 4
                        npr = n if not (last and hi == 1) else n - NSM
                        cch0 = c0 // 8
                        nch = npr // 8
                        fns = [I_acp(usT[:, tile_, :, cch0:cch0 + nch].rearrange("p s c -> p c s"),
                                     pt[:, 0:npr].rearrange("p (c s) -> p c s", s=8))]
                        if last and hi == 1:
                            fns.append(I_acp(usT[:, tile_, 0, CH:CH + NSM], pt[:, npr:n]))
                        S.op("act", fns, reads=[pkey, ZONE], writes=["usT_%d" % tile_])
                        if hi == 1:
                            ssm_shuffle_in_write(tile_, cb, last)

                    else:
                        S.op("dve", [I_cp(upT[:, col, 16 + c0:16 + c0 + n], pt[:, 0:n])], reads=[pkey, ZONE], writes=["upT%d" % col])
            W.prefetch("W")
        chk(5)
        ssm_core(b, l, cb, last)
        chk(6)
        pool_branch(b, l, nt, last)
        ssm_out(b, l, nt, cb, last)
        chk(7)
        merge_loop(b, l, nt)
        for mp in range(0, KT, 2):
            src = w_out[l].rearrange("(kt p) c -> p kt c", p=128)[:, :, mp * 128:mp * 128 + 256]
            wt, wk = W.get("W", [(lambda t: t[:], src)])
            for mi in range(2):
                m = mp + mi
                for hi, (c0, n) in enumerate(hv):
                    pt, pkey = pbank()
                    S.op("pe", [I_mm(pt[:, 0:n], wt[:, k, mi * 128:(mi + 1) * 128], mg[:, k, c0:c0 + n], k == 0, k == KT - 1)
                                for k in range(KT)], reads=[wk, "mg", ZONE] + ["usT_%d" % t_ for t_ in range(4)] + ["usT_%dz" % t_ for t_ in range(4)] + ["U8_%d" % t_ for t_ in range(4)], writes=[pkey])
                    S.op("dve", [I_tt(xT[:, m, c0:c0 + n], xT[:, m, c0:c0 + n], pt[:, 0:n], ALU.add)],
                         reads=[pkey, "xT%d" % m], writes=["xT%d" % m])
            W.prefetch("W")
        chk(8)
        fence()
        rmsnorm(V_N2 + l * 8, nt)
        ffn(b, l, nt, last)
        fence()

    def ssm_shuffle_in_write(t_, cb, last):
        uk = "usT_%d" % t_
        if last:
            S.op("dve", [I_memset(usT[:, t_, 1:8, CH:CH + NSM], 0.0)], reads=[ZONE], writes=[uk + "z"])
        S.dma("sp", scr1[t_ * 128:(t_ + 1) * 128, :, 0:cb], usT[:, t_, :, 0:cb], reads=[uk, uk + "z", ZONE], writes=["scr1_%d" % t_])
        v = scr1.rearrange("(g p) s c -> p g s c", p=16)
        for s_ in range(8):
            if s_ == 0:
                S.dma("sp", U8[16 * s_:16 * s_ + 16, 8 * t_:8 * t_ + 8, 0:cb], v[:, 8 * t_:8 * t_ + 8, s_, 0:cb],
                      reads=["scr1_%d" % t_, ZONE], writes=["U8_%d" % t_], key="U8_%d" % t_)
            else:
                S.dma_more("sp", U8[16 * s_:16 * s_ + 16, 8 * t_:8 * t_ + 8, 0:cb], v[:, 8 * t_:8 * t_ + 8, s_, 0:cb], "U8_%d" % t_)

    def ssm_shuffle_in_read(cb):
        v = scr1.rearrange("(g p) s c -> p g s c", p=16)
        for s_ in range(8):
            if s_ == 0:
                S.dma("sp", U8[16 * s_:16 * s_ + 16, :, 0:cb], v[:, :, s_, 0:cb],
                      reads=["scr1_%d" % t_ for t_ in range(4)] + [ZONE], writes=["U8_%d" % t_ for t_ in range(4)], key="U8_0")
            else:
                S.dma_more("sp", U8[16 * s_:16 * s_ + 16, :, 0:cb], v[:, :, s_, 0:cb], "U8_0")

    def ssm_core(b, l, cb, last):
        sm = sml[l]
        SKa, SKb = "sml%da" % l, "sml%db" % l
        R8, e1r, e1i = sm[:, 0, :], sm[:, 1, :], sm[:, 2, :]
        Hr, Hi_ = Hend[:, l, 0, :], Hend[:, l, 1, :]
        za, zb = tini[:, 0, :], tini[:, 1, :]
        S.op("dve", [I_tt(za, Hr, e1r, ALU.mult)], reads=["Hend%d" % l, SKa], writes=["tiniA"])
        S.op("dve", [I_tt(zb, Hi_, e1i, ALU.mult)], reads=["Hend%d" % l, SKa], writes=["tiniB"])
        S.op("dve", [I_tt(hinit[:, 0, :], za, zb, ALU.subtract)], reads=["tiniA", "tiniB"], writes=["hinit0"])
        S.op("dve", [I_tt(za, Hr, e1i, ALU.mult)], reads=["Hend%d" % l, SKa, "hinit0"], writes=["tiniA"])
        S.op("dve", [I_tt(zb, Hi_, e1r, ALU.mult)], reads=["Hend%d" % l, SKa, "hinit0"], writes=["tiniB"])
        S.op("dve", [I_tt(hinit[:, 1, :], za, zb, ALU.add)], reads=["tiniA", "tiniB"], writes=["hinit1"])
        S.op("act", [I_acp(Rt[:], R8.unsqueeze(2).to_broadcast([128, 16, CH]))],
             reads=[SKa, ZONE] + ["Gt%d%s" % (q, x) for q in range(0, 16, 2) for x in ("r", "i", "s")], writes=["Rt"])
        S.op("act", [I_memzero(Rt[:, :, 0:1])], reads=[], writes=["Rt"])
        S.op("dve", [I_tt(hiR[:], hinit[:], R8.unsqueeze(1).to_broadcast([128, 2, 16]), ALU.mult)], reads=["hinit0", "hinit1", SKa], writes=["hiR"])
        S.op("act", [I_memzero(Hp[:])], reads=[ZONE] + hT_keys(), writes=["HpZ", "ysg0", "ysg1"])
        S.op("dve", [I_cp(Hp[0:64, 0, :, :, 0], Hend[0:64, l, :, :]), I_cp(Hp[64:128, 1, :, :, 0], Hend[64:128, l, :, :])],
             reads=["Hend%d" % l, "HpZ", ZONE], writes=["Hp0"])
        for gp0 in range(0, 16, 2):
            pt, pkey = pbank()
            pv = pt[:, 0:4 * cb].rearrange("p (a r c) -> p a r c", a=2, r=2)
            fns = []
            for a in range(2):
                gp = gp0 + a
                for par in range(2):
                    g = 2 * gp + par
                    for ri_ in range(2):
                        fns.append(I_mm(pv[64 * par:64 * par + 64, a, ri_, :], Bin[l][:, g, ri_, :], U8[:, g, 0:cb], True, True))
            S.op("pe", fns, reads=["U8_%d" % (gp0 // 4), ZONE] + ["Bin%d_%d_%d" % (l, r, q) for r in range(2) for q in range(0, 16, 4)], writes=[pkey])
            Xr, Xi = pv[:, :, 0, 0:CH], pv[:, :, 1, 0:CH]
            er, ei = Er[l][:, gp0:gp0 + 2, :], Ei[l][:, gp0:gp0 + 2, :]
            gq = gp0 % 8
            t1, t2 = T1[:, gq:gq + 2, :], T2[:, gq:gq + 2, :]
            gk = "Gt%d" % gp0
            S.op("dve", [I_tt(t1, Xr, er, ALU.mult)], reads=[pkey, "Er%d" % l, ZONE], writes=["T1_%d" % gq])
            S.op("dve", [I_tt(t2, Xi, ei, ALU.mult)], reads=[pkey, "Ei%d" % l, ZONE], writes=["T2_%d" % gq])
            S.op("dve", [I_tt(Gt[:, 0, gp0:gp0 + 2, 0:CH], t1, t2, ALU.add)], reads=["T1_%d" % gq, "T2_%d" % gq, ZONE], writes=[gk + "r"])
            S.op("dve", [I_tt(t1, Xi, er, ALU.mult)], reads=[pkey, "Er%d" % l, gk + "r"], writes=["T1_%d" % gq])
            S.op("dve", [I_tt(t2, Xr, ei, ALU.mult)], reads=[pkey, "Ei%d" % l, gk + "r"], writes=["T2_%d" % gq])
            S.op("dve", [I_tt(Gt[:, 1, gp0:gp0 + 2, 0:CH], t1, t2, ALU.subtract)], reads=["T1_%d" % gq, "T2_%d" % gq], writes=[gk + "i"])
            if last:
                S.op("dve", [I_cp(X0s[:, 0, gp0:gp0 + 2, :], pv[:, :, 0, CH:cb]), I_cp(X0s[:, 1, gp0:gp0 + 2, :], pv[:, :, 1, CH:cb])],
                     reads=[pkey, ZONE], writes=[gk + "s"])
        gk_ri = ["Gt%d%s" % (q, x) for q in range(0, 16, 2) for x in "ri"]
        S.op("dve", [I_tt(Gt[:, :, :, 0], Gt[:, :, :, 0], hiR[:], ALU.add)], reads=gk_ri + ["hiR"], writes=gk_ri)
        for ri_ in range(2):
            flat = Gt[:, ri_].rearrange("p g c -> p (g c)")
            S.op("dve", [I_scan(flat, Rt[:].rearrange("p g c -> p (g c)"), flat, 0.0)], reads=gk_ri + ["Rt"], writes=gk_ri)
        gkeys = ["Gt%d%s" % (q, x) for q in range(0, 16, 2) for x in "ri"]
        t1k = ["T1_%d" % q for q in range(0, 8, 2)]
        t2k = ["T2_%d" % q for q in range(0, 8, 2)]
        for h2 in range(2):
            gsl = slice(8 * h2, 8 * h2 + 8)
            Gr, Gi = Gt[:, 0, gsl, 0:CH], Gt[:, 1, gsl, 0:CH]
            er, ei = Er[l][:, gsl, :], Ei[l][:, gsl, :]
            hk = "_%d" % h2
            S.op("dve", [I_tt(T1[:], Gr, er, ALU.mult)], reads=gkeys + ["Er%d" % l], writes=t1k)
            S.op("dve", [I_tt(T2[:], Gi, ei, ALU.mult)], reads=gkeys + ["Ei%d" % l], writes=t2k)
            S.op("dve", [I_tt(Hp[0:64, 0, 0, gsl, 1:CH], T1[0:64, :, 0:CH - 1], T2[0:64, :, 0:CH - 1], ALU.subtract),
                         I_tt(Hp[64:128, 1, 0, gsl, 1:CH], T1[64:128, :, 0:CH - 1], T2[64:128, :, 0:CH - 1], ALU.subtract)],
                 reads=t1k + t2k + [ZONE, "HpZ"], writes=["Hp1" + hk])
            S.op("dve", [I_tt(Hend[:, l, 0, gsl], T1[:, :, CH - 1], T2[:, :, CH - 1], ALU.subtract)],
                 reads=t1k + t2k + ["Hp0", "hinit0", "hinit1", "tiniA", "tiniB"], writes=["Hend%d" % l])
            S.op("dve", [I_tt(T1[:], Gr, ei, ALU.mult)], reads=gkeys + ["Ei%d" % l, "Hp1" + hk, "Hend%d" % l], writes=t1k)
            S.op("dve", [I_tt(T2[:], Gi, er, ALU.mult)], reads=gkeys + ["Er%d" % l, "Hp1" + hk, "Hend%d" % l], writes=t2k)
            S.op("dve", [I_tt(Hp[0:64, 0, 1, gsl, 1:CH], T1[0:64, :, 0:CH - 1], T2[0:64, :, 0:CH - 1], ALU.add),
                         I_tt(Hp[64:128, 1, 1, gsl, 1:CH], T1[64:128, :, 0:CH - 1], T2[64:128, :, 0:CH - 1], ALU.add)],
                 reads=t1k + t2k + ["HpZ"], writes=["Hp2" + hk])
            S.op("dve", [I_tt(Hend[:, l, 1, gsl], T1[:, :, CH - 1], T2[:, :, CH - 1], ALU.add)], reads=t1k + t2k, writes=["Hend%d" % l])
        if last:
            S.dma("sp", o_hp[:, l], Hend[:, l], reads=["Hend%d" % l], writes=["o_hp%d" % l], key="outs", out_final=True)
            S.dma("sp", h0s[:], sth_d[:, l], reads=[ZONE], writes=["h0s", "pl_a"])
            S.op("dve", [I_cp(Hp[0:64, 0, :, :, CH:cb], h0s[0:64]), I_cp(Hp[64:128, 1, :, :, CH:cb], h0s[64:128])],
                 reads=["h0s", "HpZ", ZONE], writes=["Hp3"])
            a1r, a1i, m7r, m7i = sm[:, 3, :], sm[:, 4, :], sm[:, 5, :], sm[:, 6, :]
            bb_ = lambda ap: ap.unsqueeze(2).to_broadcast([128, 16, 16])
            X0r, X0i = X0s[:, 0], X0s[:, 1]
            gsk = ["Gt%ds" % q for q in range(0, 16, 2)]
            t = [hs_t[:, i] for i in range(4)]
            S.op("dve", [I_tt(t[0], h0s[:, 0], bb_(a1r), ALU.mult), I_tt(t[1], h0s[:, 1], bb_(a1i), ALU.mult),
                         I_tt(t[2], X0r, bb_(m7r), ALU.mult), I_tt(t[3], X0i, bb_(m7i), ALU.mult)],
                 reads=["h0s", SKb, ZONE] + gsk, writes=["pl_b"])
            S.op("dve", [I_tt(t[0], t[0], t[1], ALU.subtract), I_tt(t[2], t[2], t[3], ALU.subtract)], reads=["pl_b"], writes=["pl_b"])
            S.op("dve", [I_tt(hs_new[:, 0], t[0], t[2], ALU.add)], reads=["pl_b", ZONE], writes=["hs_newr", "pl_a"])
            S.op("dve", [I_tt(t[0], h0s[:, 0], bb_(a1i), ALU.mult), I_tt(t[1], h0s[:, 1], bb_(a1r), ALU.mult),
                         I_tt(t[2], X0r, bb_(m7i), ALU.mult), I_tt(t[3], X0i, bb_(m7r), ALU.mult)],
                 reads=["h0s", SKb, "hs_newr"] + gsk, writes=["pl_b"])
            S.op("dve", [I_tt(t[0], t[0], t[1], ALU.add), I_tt(t[2], t[2], t[3], ALU.add)], reads=["pl_b"], writes=["pl_b"])
            S.op("dve", [I_tt(hs_new[:, 1], t[0], t[2], ALU.add)], reads=["pl_b"], writes=["hs_newi", "pl_a"])
            S.dma("sp", o_hs[:, l], hs_new[:], reads=["hs_newr", "hs_newi", "pl_a", ZONE], writes=["o_hs%d" % l], key="outs", out_final=True)

    def ssm_out(b, l, nt, cb, last):
        gkeys = ["Gt%d%s" % (q, x) for q in range(0, 16, 2) for x in ("r", "i", "s")]
        v2 = scr2.rearrange("(g q) j c -> q g j c", q=16)

        def wr(t_):
            for j in range(8):
                if j == 0:
                    S.dma("sp", v2[:, 8 * t_:8 * t_ + 8, j, 0:cb], Y8[16 * j:16 * j + 16, 8 * t_:8 * t_ + 8, 0:cb],
                          reads=["Y8_%d" % (8 * t_), "Y8_%d" % (8 * t_ + 4), ZONE] + gkeys, writes=["scr2_%d" % t_], key="scr2_%d" % t_)
                else:
                    S.dma_more("sp", v2[:, 8 * t_:8 * t_ + 8, j, 0:cb], Y8[16 * j:16 * j + 16, 8 * t_:8 * t_ + 8, 0:cb], "scr2_%d" % t_)

        def rd(t_):
            S.dma("sp", ysT[:, t_, :, 0:cb], scr2[t_ * 128:(t_ + 1) * 128, :, 0:cb], reads=["scr2_%d" % t_, ZONE] + gkeys, writes=["ysT_%d" % t_])

        for g0 in range(0, 32, 4):
            h2 = (g0 // 2) // 8
            hpk = ["HpZ", "Hp0", "Hp1_%d" % h2, "Hp2_%d" % h2] + (["Hp3"] if last else [])
            pt, pkey = pbank()
            fns = []
            for gg in range(4):
                g = g0 + gg
                gp, par = g // 2, g % 2
                o = pt[:, gg * 128:gg * 128 + cb]
                fns.append(I_mm(o, Toep[l][:, g, :], U8[:, g, 0:cb], True, False))
                fns.append(I_mm(o, Cout[l][:, gp, 0, :], Hp[:, par, 0, gp, 0:cb], False, False))
                fns.append(I_mm(o, Cout[l][:, gp, 1, :], Hp[:, par, 1, gp, 0:cb], False, True))
            S.op("pe", fns, reads=["U8_%d" % (g0 // 8), ZONE, "Toep%d_%d" % (l, g0), "Cout%dr" % l, "Cout%di" % l] + hpk, writes=[pkey])
            S.op("act", [I_acp(Y8[:, g0:g0 + 4, 0:cb], pt[:].rearrange("p (g c) -> p g c", g=4)[:, :, 0:cb])],
                 reads=[pkey, ZONE], writes=["Y8_%d" % g0, "Rt"] + gkeys)
            if g0 == 28:
                y8all = ["Y8_%d" % q for q in range(0, 32, 4)]
                for j in range(8):
                    if j == 0:
                        S.dma("sp", v2[:, :, j, 0:cb], Y8[16 * j:16 * j + 16, :, 0:cb], reads=y8all + [ZONE] + gkeys, writes=["scr2_0"], key="scr2_0")
                    else:
                        S.dma_more("sp", v2[:, :, j, 0:cb], Y8[16 * j:16 * j + 16, :, 0:cb], "scr2_0")
                for t2 in range(4):
                    S.dma("sp", ysT[:, t2, :, 0:cb], scr2[t2 * 128:(t2 + 1) * 128, :, 0:cb], reads=["scr2_0", ZONE] + gkeys, writes=["ysT_%d" % t2])
        for tl in range(4):
            S.op("dve", [I_stt(ysT[:, tl, :, 0:cb], usT[:, tl, :, 0:cb], V(V_SSD + l * 4 + tl), ysT[:, tl, :, 0:cb], ALU.mult, ALU.add)],
                 reads=["ysT_%d" % tl, "usT_%d" % tl, "usT_%dz" % tl, "vecs", ZONE] + gkeys, writes=["ysT%d" % tl])
            fns = [I_act(gy[:, tl, 0:TBP].rearrange("p (c s) -> p c s", s=8), ysT[:, tl, :, 0:CH].rearrange("p s c -> p c s"),
                         AF.Gelu_apprx_tanh)]
            if last:
                fns.append(I_act(gy[:, tl, TBP:TBP + NSM], ysT[:, tl, 0, CH:cb], AF.Gelu_apprx_tanh))
            S.op("act", fns, reads=["ysT%d" % tl, ZONE] + gkeys, writes=["gy%d" % tl])

    def pool_branch(b, l, nt, last):
        npr = TBP
        for g in range(4):
            w = 2 << g
            uk = "upT%d" % g
            ext = upT[:, g, :]
            S.op("dve", [I_cp(ext[:, 0:16], uhist[:, l, g, :])], reads=["uhist%d_%d" % (l, g), ZONE], writes=[uk + "h"])
            cur_ = ext
            n_steps = g + 1
            bufs = [cbf[0], cbf[1]] if False else None
            sc_a = sg[0][:].rearrange("p a n -> p (a n)")
            sc_b = sg[1][:].rearrange("p a n -> p (a n)")
            src = ext
            lo = 0
            for st in range(n_steps):
                sh = 1 << st
                dst = sc_a if st % 2 == 0 else sc_b
                S.op("dve", [I_tt(dst[:, lo + sh:16 + npr], src[:, lo + sh:16 + npr], src[:, lo:16 + npr - sh], ALU.add)],
                     reads=[uk, uk + "h", "pl_a", "pl_b", ZONE], writes=["pl_a" if st % 2 == 0 else "pl_b"])
                src = dst
                lo += sh
            S.op("dve", [I_stt(dT[:, g, 0:npr], src[:, 16:16 + npr], 1.0 / w, ext[:, 16:16 + npr], ALU.mult, ALU.subtract)],
                 reads=["pl_a", "pl_b", uk, ZONE], writes=["dT%d" % g])
            if b == 0:
                free_ = sc_a if (n_steps % 2 == 0) else sc_b
                fk = "pl_a" if (n_steps % 2 == 0) else "pl_b"
                S.op("dve", [I_tt(free_[:, 0:16], src[:, 16:32], invc[:, g, :], ALU.mult)],
                     reads=["pl_a", "pl_b", "invc", "dT%d" % g, ZONE], writes=[fk])
                S.op("dve", [I_tt(dT[:, g, 0:16], free_[:, 0:16], ext[:, 16:32], ALU.subtract)],
                     reads=[fk, uk, ZONE], writes=["dT%d" % g])
            S.op("dve", [I_cp(uhist[:, l, g, :], ext[:, npr:npr + 16])], reads=[uk, uk + "h"], writes=["uhist%d_%d" % (l, g)])
            if last:
                S.dma("sp", o_pp[:, l, g, :], uhist[:, l, g, 1:16], reads=["uhist%d_%d" % (l, g)], writes=["o_pp%d_%d" % (l, g)], key="outs", out_final=True)
        if last:
            S.dma("sp", hsT[:], stp_d[:, l], reads=[ZONE], writes=["hsT"])
            for g in range(4):
                w = 2 << g
                us = upT[:, g, 16 + TBP:16 + TBP + NSM]
                S.op("dve", [I_red(hsum[:, g, :], hsT[:, g, :, 15 - (w - 1):15])], reads=["hsT", ZONE], writes=["hsum%d" % g])
                S.op("dve", [I_tt(hsum[:, g, :], hsum[:, g, :], us, ALU.add)], reads=["hsum%d" % g, "upT%d" % g], writes=["hsum%db" % g])
                S.op("dve", [I_stt(dT[:, g, TBP:TBP + NSM], hsum[:, g, :], 1.0 / w, us, ALU.mult, ALU.subtract)],
                     reads=["hsum%db" % g, "upT%d" % g], writes=["dT%d" % g])
                S.op("dve", [I_cp(usn[:, g, :], us)], reads=["upT%d" % g, ZONE], writes=["usn%d" % g])
                S.dma("sp", o_psn[:, l, g, :], usn[:, g, :], reads=["usn%d" % g, ZONE], writes=["o_psn%d_%d" % (l, g)], key="outs", out_final=True)

    def merge_loop(b, l, nt):
        hv = halves(nt)
        srcp = pool_w[l].rearrange("g p c -> p g c")
        wpt, wpk = W.get("P", [(lambda t: t[:], srcp)])
        for m in range(KT):
            wt, wk = W.get("W", [(lambda t: t[:, :, 0:128], w_in[l].rearrange("(kt p) c -> p kt c", p=128)[:, :, 1024 + m * 128:1024 + (m + 1) * 128]),
                                 (lambda t: t[:, :, 128:256], w_in[l].rearrange("(kt p) c -> p kt c", p=128)[:, :, 2048 + m * 128:2048 + (m + 1) * 128])])
            gt, gk = W.get("G", [(lambda t: t[:, :, 0:128], w_glu[l].rearrange("(kt p) c -> p kt c", p=128)[:, :, m * 128:(m + 1) * 128]),
                                 (lambda t: t[:, :, 128:256], w_glu[l].rearrange("(kt p) c -> p kt c", p=128)[:, :, 1024 + m * 128:1024 + (m + 1) * 128])])
            for hi, (c0, n) in enumerate(hv):
                sl = slice(c0, c0 + n)
                p_gp, k_gp = pbank()
                p_gs, k_gs = pbank()
                p_yp, k_yp = pbank()
                p_z1, k_z1 = pbank()
                p_z2, k_z2 = pbank()
                S.op("pe", [I_mm(p_gp[:, 0:n], wt[:, k, 0:128], hT[:, k, sl], k == 0, k == KT - 1) for k in range(KT)],
                     reads=[wk] + hT_keys(), writes=[k_gp])
                S.op("pe", [I_mm(p_gs[:, 0:n], wt[:, k, 128:256], hT[:, k, sl], k == 0, k == KT - 1) for k in range(KT)],
                     reads=[wk] + hT_keys(), writes=[k_gs])
                S.op("pe", [I_mm(p_yp[:, 0:n], wpt[:, m // 2, (m % 2) * 128:(m % 2) * 128 + 128], dT[:, m // 2, sl], True, True)],
                     reads=[wpk, ZONE, "dT%d" % (m // 2)], writes=[k_yp])
                S.op("pe", [I_mm(p_z1[:, 0:n], gt[:, k, 0:128], gy[:, k, sl], k == 0, k == 3) for k in range(4)],
                     reads=[gk, ZONE] + ["gy%d" % k for k in range(4)], writes=[k_z1])
                S.op("pe", [I_mm(p_z2[:, 0:n], gt[:, k, 128:256], gy[:, k, sl], k == 0, k == 3) for k in range(4)],
                     reads=[gk, ZONE] + ["gy%d" % k for k in range(4)], writes=[k_z2])
                sgt = sg[hi]
                S.op("act", I_act(sgt[:, 0, 0:n], p_gp[:, 0:n], AF.Sigmoid), reads=[k_gp, ZONE], writes=["sg%d_0" % hi, "pl_a" if hi == 0 else "pl_b"])
                S.op("act", I_act(sgt[:, 1, 0:n], p_gs[:, 0:n], AF.Sigmoid), reads=[k_gs, ZONE], writes=["sg%d_1" % hi, "pl_a" if hi == 0 else "pl_b"])
                S.op("act", I_act(sgt[:, 2, 0:n], p_z2[:, 0:n], AF.Sigmoid), reads=[k_z2, ZONE], writes=["sg%d_2" % hi, "pl_a" if hi == 0 else "pl_b"])
                S.op("dve", [I_stt(sgt[:, 0, 0:n], p_yp[:, 0:n], V(V_PSC + l * 8 + m), sgt[:, 0, 0:n], ALU.mult, ALU.mult)],
                     reads=[k_yp, "sg%d_0" % hi, "vecs", ZONE], writes=["sg%d_0" % hi])
                S.op("dve", [I_tt(sgt[:, 2, 0:n], p_z1[:, 0:n], sgt[:, 2, 0:n], ALU.mult)], reads=[k_z1, "sg%d_2" % hi, ZONE], writes=["sg%d_2" % hi])
                S.op("dve", [I_tt(sgt[:, 2, 0:n], sgt[:, 2, 0:n], sgt[:, 1, 0:n], ALU.mult)], reads=["sg%d_2" % hi, "sg%d_1" % hi], writes=["sg%d_2" % hi])
                S.op("dve", [I_tt(mg[:, m, sl], sgt[:, 0, 0:n], sgt[:, 2, 0:n], ALU.add)],
                     reads=["sg%d_0" % hi, "sg%d_2" % hi, ZONE], writes=["mg"] + ["usT_%d" % t_ for t_ in range(4)] + ["usT_%dz" % t_ for t_ in range(4)] + ["U8_%d" % t_ for t_ in range(4)])
            W.prefetch("W")
            W.prefetch("G")
        W.prefetch("P")

    def ffn(b, l, nt, last):
        hv = halves(nt)
        npr = TBP
        if last:
            S.dma("sp", stcs[:], stc_d[:, l], reads=[ZONE, "mg"], writes=["stcs"])
        for j in range(FT):
            wt, wk = W.get("W", [(lambda t: t[:, :, 0:128], w_up[l].rearrange("(kt p) c -> p kt c", p=128)[:, :, j * 128:(j + 1) * 128]),
                                 (lambda t: t[:, :, 128:256], w_up[l].rearrange("(kt p) c -> p kt c", p=128)[:, :, DFF + j * 128:DFF + (j + 1) * 128])])
            gs_, gsk = gS[j % 2], "gS%d" % (j % 2)
            cb_, cbk = cbf[j % 2], "cbf%d" % (j % 2)
            gc_, gck = gcb[j % 2], "gcb%d" % (j % 2)
            vs_, vsk = vS[j % 2], "vS%d" % (j % 2)
            for hi, (c0, n) in enumerate(hv):
                sl = slice(c0, c0 + n)
                p_g, k_g = pbank()
                p_v, k_v = pbank()
                if j == 0:
                    for k in range(KT):
                        S.op("pe", [I_mm(p_g[:, 0:n], wt[:, k, 0:128], hT[:, k, sl], k == 0, k == KT - 1)], reads=[wk, "hT%d_h%d" % (k, hi)], writes=[k_g])
                else:
                    S.op("pe", [I_mm(p_g[:, 0:n], wt[:, k, 0:128], hT[:, k, sl], k == 0, k == KT - 1) for k in range(KT)],
                         reads=[wk] + hT_keys(), writes=[k_g])
                S.op("pe", [I_mm(p_v[:, 0:n], wt[:, k, 128:256], hT[:, k, sl], k == 0, k == KT - 1) for k in range(KT)],
                     reads=[wk] + hT_keys(), writes=[k_v])
                S.op("act", [I_acp(gs_[:, 2 + c0:2 + c0 + n], p_g[:, 0:n])], reads=[k_g, ZONE], writes=[gsk + "_%d" % hi])
                S.op("act", [I_acp(vs_[:, c0:c0 + n], p_v[:, 0:n])], reads=[k_v, ZONE], writes=[vsk + "_%d" % hi])
            W.prefetch("W")
            gk2 = [gsk + "_0", gsk + "_1"]
            S.op("dve", [I_cp(gs_[:, 0:2], ghist[:, l, j, :])], reads=["ghist%d_%d" % (l, j), ZONE], writes=[gsk + "_h"])
            allg = gk2 + [gsk + "_h"]
            cw = lambda k: V(V_CW + (l * 3 + k) * FT + j)
            cbias = V(V_CB + l * FT + j)
            S.op("act", I_act(cb_[:, 0:npr], gs_[:, 0:npr], AF.Identity, scale=cw(0), bias=cbias), reads=allg + ["vecs", ZONE], writes=[cbk])
            S.op("dve", [I_stt(cb_[:, 0:npr], gs_[:, 1:1 + npr], cw(1), cb_[:, 0:npr], ALU.mult, ALU.add)], reads=allg + [cbk, "vecs"], writes=[cbk + "b"])
            S.op("dve", [I_stt(cb_[:, 0:npr], gs_[:, 2:2 + npr], cw(2), cb_[:, 0:npr], ALU.mult, ALU.add)], reads=allg + [cbk + "b"], writes=[cbk + "c"])
            ckeys = [cbk + "c"]
            if last:
                cs = cb_[:, npr:npr + NSM]
                S.op("dve", [I_ts(cs, stcs[:, j, :, 0], cw(0), ALU.mult, cbias, ALU.add)], reads=["stcs", "vecs", ZONE, cbk + "c"], writes=[cbk + "s"])
                S.op("dve", [I_stt(cs, stcs[:, j, :, 1], cw(1), cs, ALU.mult, ALU.add)], reads=["stcs", cbk + "s"], writes=[cbk + "s2"])
                S.op("dve", [I_stt(cs, gs_[:, 2 + npr:2 + npr + NSM], cw(2), cs, ALU.mult, ALU.add)], reads=allg + [cbk + "s2"], writes=[cbk + "s3"])
                ckeys.append(cbk + "s3")
                S.dma("sp", o_csn[:, l, j, :], gs_[:, 2 + npr:2 + npr + NSM], reads=allg + [ZONE], writes=["o_csn%d_%d" % (l, j)], key="outs", out_final=True)
            S.op("dve", [I_cp(ghist[:, l, j, :], gs_[:, npr:npr + 2])], reads=allg, writes=["ghist%d_%d" % (l, j)])
            if last:
                S.dma("sp", o_cp[:, l, j, :], ghist[:, l, j, :], reads=["ghist%d_%d" % (l, j)], writes=["o_cp%d_%d" % (l, j)], key="outs", out_final=True)
            S.op("act", I_act(gc_[:, 0:nt], cb_[:, 0:nt], AF.Gelu_apprx_tanh), reads=ckeys + [ZONE], writes=[gck])
            S.op("dve", [I_tt(a_ff[:, j, 0:nt], gc_[:, 0:nt], vs_[:, 0:nt], ALU.mult)], reads=[gck, vsk + "_0", vsk + "_1", ZONE], writes=["a_ff%d" % j])
        for mp in range(0, KT, 2):
            src = w_down[l].rearrange("(kt p) c -> p kt c", p=128)[:, :, mp * 128:(mp + 2) * 128]
            wt, wk = W.get("D", [(lambda t: t[:], src)])
            for mi in range(2):
                m = mp + mi
                cs = slice(mi * 128, (mi + 1) * 128)
                for hi, (c0, n) in enumerate(hv):
                    pt, pkey = pbank()
                    if m == 0:
                        S.op("pe", [I_mm(pt[:, 0:n], wt[:, k, cs], a_ff[:, k, c0:c0 + n], k == 0, False) for k in range(FT - 2)],
                             reads=[wk, ZONE] + ["a_ff%d" % k for k in range(FT - 2)], writes=[pkey])
                        S.op("pe", [I_mm(pt[:, 0:n], wt[:, k, cs], a_ff[:, k, c0:c0 + n], False, k == FT - 1) for k in range(FT - 2, FT)],
                             reads=[wk, ZONE] + ["a_ff%d" % k for k in range(FT - 2, FT)], writes=[pkey])
                    else:
                        S.op("pe", [I_mm(pt[:, 0:n], wt[:, k, cs], a_ff[:, k, c0:c0 + n], k == 0, k == FT - 1) for k in range(FT)],
                             reads=[wk, ZONE] + ["a_ff%d" % k for k in range(FT)], writes=[pkey])
                    S.op("dve", [I_tt(xT[:, m, c0:c0 + n], xT[:, m, c0:c0 + n], pt[:, 0:n], ALU.add)],
                         reads=[pkey, "xT%d" % m], writes=["xT%d" % m])
            W.prefetch("D")

    def final_out(b, nt):
        t0 = b * TBP
        nb = b + 1
        for m in range(KT):
            if m % 2:
                S.op("act", [I_acp(yT[:, m, 0:nt], xT[:, m, 0:nt])], reads=["xT%d" % m, ZONE], writes=["yTm%d" % m])
            else:
                S.op("dve", [I_cp(yT[:, m, 0:nt], xT[:, m, 0:nt])], reads=["xT%d" % m, ZONE], writes=["yTm%d" % m])
            if nb < NBLK:
                nt2 = TBP + (NSM if nb == NBLK - 1 else 0)
                S.dma("sp", xT[:, m, 0:nt2], x_tT[m * 128:(m + 1) * 128, nb * TBP:nb * TBP + nt2], writes=["xT%d" % m], key="xload")
        hv_ = norm_stats(nt, src=yT, skey="yTm%d")
        for hi, (c0, n) in enumerate(hv_):
            S.op("dve", [I_recip(rstd[:, c0:c0 + n], rt[:, c0:c0 + n])], reads=["rt%d" % hi, ZONE], writes=["rstd%d" % hi])
        for m in range(KT):
            st, sk = ysg[m % 2], "ysg%d" % (m % 2)
            S.op("dve", [I_stt(st[:, c0:c0 + n], yT[:, m, c0:c0 + n], V(V_NF + m), rstd[:, c0:c0 + n], ALU.mult, ALU.mult) for (c0, n) in hv_],
                 reads=["yTm%d" % m, "rstd0", "rstd1", "vecs", "HpZ", ZONE], writes=[sk])
            S.dma("sp", y_allT[m * 128:(m + 1) * 128, t0:t0 + nt], st[:, 0:nt], reads=[sk], writes=["y_all"], key="outs", out_final=True)
        fence()

    S.limit = limit
    S.dry = True
    S.calls = 0
    try:
        gen()
    except StopGen:
        pass
    S.dry = False
    S.calls = 0
    try:
        gen()
    except StopGen:
        pass
    S.emit()
    es.close()
    return nc


def _host_layouts(inp):
    f32 = np.float32
    c = {}
    def fm(v, ntile):
        v = np.asarray(v, f32)
        lead = v.shape[:-1]
        return np.ascontiguousarray(np.moveaxis(v.reshape(lead + (ntile, 128)), -1, 0))
    vecs = np.zeros((128, V_END), f32)
    vecs[:, V_N1:V_N1 + 16] = fm(inp["norm1_g"], 8).reshape(128, 16)
    vecs[:, V_N2:V_N2 + 16] = fm(inp["norm2_g"], 8).reshape(128, 16)
    vecs[:, V_NF:V_NF + 8] = fm(inp["norm_f_g"], 8).reshape(128, 8)
    vecs[:, V_PSC:V_PSC + 16] = fm(inp["pool_scale"], 8).reshape(128, 16)
    vecs[:, V_SSD:V_SSD + 8] = fm(inp["ssm_D"], 4).reshape(128, 8)
    vecs[:, V_CW:V_CW + 132] = fm(inp["conv_w"], FT).reshape(128, 132)
    vecs[:, V_CB:V_CB + 44] = fm(inp["conv_b"], FT).reshape(128, 44)
    c["vecs"] = vecs
    def gl(a):
        a = np.asarray(a, f32)
        rest = a.shape[3:]
        a = a.reshape((L, 16, 2, 64) + rest)
        a = np.moveaxis(a, (2, 3), (0, 1))
        return np.ascontiguousarray(a.reshape((128, L, 16) + rest))
    A_re, A_im = gl(inp["ssm_A_re"]), gl(inp["ssm_A_im"])
    ldt = gl(np.broadcast_to(np.asarray(inp["ssm_log_dt"], f32)[:, :, None], (L, 32, 64)))
    c["ssmA"] = np.ascontiguousarray(np.stack([A_re, A_im, ldt], axis=2))
    c["ssmB"] = np.ascontiguousarray(np.stack([gl(inp["ssm_B_re"]), gl(inp["ssm_B_im"])], axis=2))
    Ct = lambda a: gl(np.swapaxes(np.asarray(a, f32), 2, 3))
    c["ssmC"] = np.ascontiguousarray(np.stack([Ct(inp["ssm_C_re"]), Ct(inp["ssm_C_im"])], axis=2))
    c["ident"] = np.eye(128, dtype=f32)
    s_idx = np.arange(128) // 16
    c["mask"] = (s_idx[None, :] >= s_idx[:, None]).astype(f32)
    c["kvec"] = np.ascontiguousarray(np.broadcast_to(np.asarray(KLIST, f32)[None, None, :], (128, 16, NK)))
    c["cvec"] = np.ascontiguousarray(np.broadcast_to(np.arange(CH, dtype=f32)[None, None, :], (128, 16, CH)))
    invc = np.zeros((128, 4, 16), f32)
    for g in range(4):
        invc[:, g, :] = 1.0 / np.minimum(2 << g, np.arange(16) + 1)
    c["invc"] = invc
    return c


_PROG = None


def make_in_maps(inp, cores=range(NCORES)):
    f32 = np.float32
    common = _host_layouts(inp)
    for k in ("w_in", "pool_w", "w_glu", "w_out", "w_up", "w_down"):
        common[k] = np.ascontiguousarray(np.asarray(inp[k], f32))
    xp = np.asarray(inp["x_prompt"], f32)
    xs = np.asarray(inp["x_sample"], f32)
    meta = np.asarray(inp["meta_tokens"], f32)
    st_pool = np.asarray(inp["state_pool"], f32)
    st_re = np.asarray(inp["state_ssm_re"], f32)
    st_im = np.asarray(inp["state_ssm_im"], f32)
    st_conv = np.asarray(inp["state_conv"], f32)
    in_maps = []
    for i in cores:
        m = dict(common)
        bs = slice(NSM * i, NSM * (i + 1))
        m["x_tT"] = np.ascontiguousarray(np.concatenate([meta, xp[i], xs[bs, 0, :]], axis=0).T)
        sp = st_pool[:, bs]
        m["sp_raw"] = np.ascontiguousarray(sp)
        m["stp"] = np.ascontiguousarray(sp.reshape(L, NSM, 15, 4, 128).transpose(4, 0, 3, 1, 2))
        sc = st_conv[:, bs]
        m["sc_raw"] = np.ascontiguousarray(sc)
        m["stc"] = np.ascontiguousarray(sc.reshape(L, NSM, 2, FT, 128).transpose(4, 0, 3, 1, 2))
        h = np.stack([st_re[:, bs], st_im[:, bs]], axis=0)
        h = h.reshape(2, L, NSM, 16, 2, 64).transpose(4, 5, 1, 0, 3, 2)
        m["sth"] = np.ascontiguousarray(h.reshape(128, L, 2, 16, NSM))
        in_maps.append(m)
    return in_maps


def assemble_core(r):
    f32 = np.float32
    o = {}
    y_all = np.ascontiguousarray(r["y_allT"].T)
    o["y_prompt"] = y_all[NMETA:NPR]
    o["y_sample"] = y_all[NPR:NTOK][:, None, :]
    o["pool_p"] = r["o_pp"].transpose(1, 3, 2, 0).reshape(L, 15, 512)
    a = r["o_hp"].reshape(2, 64, L, 2, 16).transpose(2, 3, 4, 0, 1).reshape(L, 2, 32, 64)
    o["re_p"], o["im_p"] = a[:, 0], a[:, 1]
    o["conv_p"] = r["o_cp"].transpose(1, 3, 2, 0).reshape(L, 2, DFF)
    new_row = r["o_psn"].transpose(1, 3, 2, 0).reshape(L, NSM, 1, 512)
    o["pool_s"] = np.concatenate([r["o_psh"][:, :, 1:15, :], new_row], axis=2)
    a = r["o_hs"].reshape(2, 64, L, 2, 16, NSM).transpose(2, 3, 5, 4, 0, 1).reshape(L, 2, NSM, 32, 64)
    o["re_s"], o["im_s"] = a[:, 0], a[:, 1]
    newc = r["o_csn"].transpose(1, 3, 2, 0).reshape(L, NSM, 1, DFF)
    o["conv_s"] = np.concatenate([r["o_csh"][:, :, 1:2, :], newc], axis=2)
    return o


def kernel(**inp):
    global _PROG
    f32 = np.float32
    if _PROG is None:
        _PROG = build_program()
    nc = _PROG
    in_maps = make_in_maps(inp)
    res = run_bass_kernel_spmd(nc, in_maps, core_ids=list(range(NCORES)))
    P = [assemble_core(r) for r in res.results]
    y_prompt = np.stack([p["y_prompt"] for p in P], axis=0)
    y_sample = np.concatenate([p["y_sample"] for p in P], axis=0)
    stk = lambda k: np.stack([p[k] for p in P], axis=1)
    cat = lambda k: np.concatenate([p[k] for p in P], axis=1)
    outs = (y_prompt, y_sample, stk("pool_p"), stk("re_p"), stk("im_p"), stk("conv_p"),
            cat("pool_s"), cat("re_s"), cat("im_s"), cat("conv_s"))
    return tuple(np.ascontiguousarray(o, dtype=f32) for o in outs)
```
